# Optimizing a Trainium2 kernel written in Bass

```python
import math
import jax
import jax.numpy as jnp
from jax import lax
import numpy as np

D_MODEL = 1024
BATCH = 32
SEQ = 256
DEPTH = 4
DEC_BATCH = 8
DEC_SEQ = 2048
PAST_LEN = 512

GRID_W = 64
N_HEADS = 16
HEAD_DIM = D_MODEL // N_HEADS
N_KV_A = 16
N_KV_GQA = 4
D_FF = 4 * D_MODEL
N_MIXERS = 3
Q_BLOCK = 128
WINDOW = 128
WIN_H = 8
WIN_W = 16
NA_QCOLS = 16
NA_KCOLS = 32
ROPE_BASE = 10000.0
EPS = 1e-6
NEG_INF = -1e30
ADA_CHUNKS = 6

kernel_name = 'hybrid_diffusion_prefix_step'


def rmsnorm(x, g):
    xf = x.astype(jnp.float32)
    y = xf * lax.rsqrt(jnp.mean(xf * xf, axis=-1, keepdims=True) + EPS)
    return (y * g.astype(jnp.float32)).astype(x.dtype)


def rope_1d(x, pos):
    half = x.shape[-1] // 2
    freqs = jnp.exp(-math.log(ROPE_BASE) * jnp.arange(half, dtype=jnp.float32) / half)
    ang = pos[:, None] * freqs[None, :]
    shape = (x.shape[1],) + (1,) * (x.ndim - 3) + (half,)
    cos = jnp.cos(ang).reshape(shape).astype(x.dtype)
    sin = jnp.sin(ang).reshape(shape).astype(x.dtype)
    x1, x2 = x[..., :half], x[..., half:]
    return jnp.concatenate([x1 * cos - x2 * sin, x2 * cos + x1 * sin], axis=-1)


def rope_2d(x):
    t = jnp.arange(x.shape[1])
    rows = (t // GRID_W).astype(jnp.float32)
    cols = (t % GRID_W).astype(jnp.float32)
    half = x.shape[-1] // 2
    return jnp.concatenate([rope_1d(x[..., :half], rows), rope_1d(x[..., half:], cols)], axis=-1)


def split_qkv(qkv, n_kv):
    b_, t_ = qkv.shape[:2]
    nq = N_HEADS * HEAD_DIM
    nk = n_kv * HEAD_DIM
    q = qkv[..., :nq].reshape(b_, t_, n_kv, N_HEADS // n_kv, HEAD_DIM)
    k = qkv[..., nq:nq + nk].reshape(b_, t_, n_kv, HEAD_DIM)
    v = qkv[..., nq + nk:].reshape(b_, t_, n_kv, HEAD_DIM)
    return q, k, v


def attend(q, k, v, mask=None, sink=None):
    s = jnp.einsum('bqhgd,bkhd->bhgqk', q, k).astype(jnp.float32) * (HEAD_DIM ** -0.5)
    if mask is not None:
        s = jnp.where(mask, s, NEG_INF)
    if sink is not None:
        col = jnp.broadcast_to(sink.astype(jnp.float32)[None, :, :, None, None], s.shape[:-1] + (1,))
        p = jax.nn.softmax(jnp.concatenate([s, col], axis=-1), axis=-1)[..., :-1]
    else:
        p = jax.nn.softmax(s, axis=-1)
    return jnp.einsum('bhgqk,bkhd->bqhgd', p.astype(v.dtype), v)


def dense_blocked(q, k, v, sink=None):
    b_, t_ = q.shape[:2]
    nb = t_ // Q_BLOCK
    qb = jnp.moveaxis(q.reshape((b_, nb, Q_BLOCK) + q.shape[2:]), 1, 0)
    ob = lax.map(lambda blk: attend(blk, k, v, sink=sink), qb)
    return jnp.moveaxis(ob, 0, 1).reshape(q.shape)


def windowed_sink_attention(q, k, v, k_ctx, v_ctx, sink):
    b_, t_ = q.shape[:2]
    nb = t_ // Q_BLOCK
    n_ctx = k_ctx.shape[1]
    pad = ((0, 0), (Q_BLOCK, Q_BLOCK), (0, 0), (0, 0))
    kp, vp = jnp.pad(k, pad), jnp.pad(v, pad)
    qb = jnp.moveaxis(q.reshape((b_, nb, Q_BLOCK) + q.shape[2:]), 1, 0)
    ctx_mask = jnp.ones((Q_BLOCK, n_ctx), dtype=bool)

    def blk(args):
        i, q_blk = args
        start = i * Q_BLOCK
        k_band = lax.dynamic_slice_in_dim(kp, start, 3 * Q_BLOCK, axis=1)
        v_band = lax.dynamic_slice_in_dim(vp, start, 3 * Q_BLOCK, axis=1)
        qpos = start + jnp.arange(Q_BLOCK)
        kpos = start - Q_BLOCK + jnp.arange(3 * Q_BLOCK)
        valid = ((kpos >= 0) & (kpos < t_))[None, :] & (jnp.abs(qpos[:, None] - kpos[None, :]) <= WINDOW)
        mask = jnp.concatenate([valid, ctx_mask], axis=1)
        return attend(q_blk, jnp.concatenate([k_band, k_ctx], axis=1),
                      jnp.concatenate([v_band, v_ctx], axis=1), mask=mask, sink=sink)

    ob = lax.map(blk, (jnp.arange(nb), qb))
    return jnp.moveaxis(ob, 0, 1).reshape(q.shape)


def neighbourhood_attention(q, k, v, k_ctx, v_ctx, rpb):
    b_, t_, h_, d_ = q.shape
    rows = t_ // GRID_W
    kh = min(WIN_H, rows)
    ncb = GRID_W // NA_QCOLS
    scale = HEAD_DIM ** -0.5
    qc = np.arange(GRID_W).reshape(ncb, NA_QCOLS)
    band0 = np.clip(np.arange(ncb) * NA_QCOLS - WIN_W // 2, 0, GRID_W - NA_KCOLS)
    kc = band0[:, None] + np.arange(NA_KCOLS)
    c0 = np.clip(qc - WIN_W // 2, 0, GRID_W - WIN_W)
    col_valid = (kc[:, None, :] >= c0[:, :, None]) & (kc[:, None, :] < c0[:, :, None] + WIN_W)
    col_idx = np.clip(kc[:, None, :] - qc[:, :, None] + WIN_W - 1, 0, 2 * WIN_W - 2)
    loc_mask = jnp.asarray(col_valid)[:, :, None, :]
    rpb_c = rpb[:, :, col_idx]
    kg = k.reshape(b_, rows, GRID_W, h_, d_)
    vg = v.reshape(b_, rows, GRID_W, h_, d_)
    qrows = jnp.moveaxis(q.reshape(b_, rows, GRID_W, h_, d_), 1, 0).reshape(rows, b_, ncb, NA_QCOLS, h_, d_)
    n_loc = kh * NA_KCOLS

    def row(args):
        r, q_r = args
        r0 = jnp.clip(r - kh // 2, 0, rows - kh)
        kb = jnp.take(lax.dynamic_slice_in_dim(kg, r0, kh, axis=1), kc, axis=2)
        vb = jnp.take(lax.dynamic_slice_in_dim(vg, r0, kh, axis=1), kc, axis=2)
        row_idx = r0 + jnp.arange(kh) - r + WIN_H - 1
        bias = jnp.transpose(jnp.take(rpb_c, row_idx, axis=1), (0, 2, 3, 1, 4))
        s_loc = jnp.einsum('bnqhd,binchd->bhnqic', q_r, kb).astype(jnp.float32) * scale + bias.astype(jnp.float32)
        s_loc = jnp.where(loc_mask, s_loc, NEG_INF).reshape(b_, h_, ncb, NA_QCOLS, n_loc)
        s_ctx = jnp.einsum('bnqhd,bkhd->bhnqk', q_r, k_ctx).astype(jnp.float32) * scale
        p = jax.nn.softmax(jnp.concatenate([s_loc, s_ctx], axis=-1), axis=-1).astype(v.dtype)
        p_loc = p[..., :n_loc].reshape(b_, h_, ncb, NA_QCOLS, kh, NA_KCOLS)
        o = (jnp.einsum('bhnqic,binchd->bnqhd', p_loc, vb)
             + jnp.einsum('bhnqk,bkhd->bnqhd', p[..., n_loc:], v_ctx))
        return o.reshape(b_, GRID_W, h_, d_)

    o = lax.map(row, (jnp.arange(rows), qrows))
    return jnp.moveaxis(o, 0, 1).reshape(b_, t_, h_, d_)


def sublayers(x, cond, l, mix, ada_w, ada_b, norm_mix_g, norm_mlp_g, w_o, mlp_w1, mlp_b1, mlp_w2, mlp_b2):
    mods = jnp.split(jax.nn.silu(cond) @ ada_w[l] + ada_b[l], ADA_CHUNKS, axis=-1)
    sh_a, sc_a, g_a, sh_f, sc_f, g_f = [m_[:, None, :] for m_ in mods]
    h = rmsnorm(x, norm_mix_g[l]) * (1 + sc_a) + sh_a
    o, kv = mix(h)
    x = x + g_a * (o.reshape(x.shape) @ w_o[l])
    h = rmsnorm(x, norm_mlp_g[l]) * (1 + sc_f) + sh_f
    x = x + g_f * (jnp.square(jax.nn.relu(h @ mlp_w1[l] + mlp_b1[l])) @ mlp_w2[l] + mlp_b2[l])
    return x, kv


def setup_inputs(seed: int = 0) -> dict:
    key = jax.random.key(seed)
    ks = iter(jax.random.split(key, 32))

    def nrm(shape, s):
        return s * jax.random.normal(next(ks), shape, jnp.float32)

    n_a, n_b, n_c = (len(range(m, DEPTH, N_MIXERS)) for m in range(N_MIXERS))
    qkv_a = (N_HEADS + 2 * N_KV_A) * HEAD_DIM
    qkv_g = (N_HEADS + 2 * N_KV_GQA) * HEAD_DIM
    fan = D_MODEL ** -0.5
    return {
        'x_prompt': nrm((BATCH, SEQ, D_MODEL), 1.0),
        'x_sample': nrm((DEC_BATCH, DEC_SEQ, D_MODEL), 1.0),
        'cache_k_a': nrm((DEC_BATCH, n_a, PAST_LEN, N_KV_A, HEAD_DIM), 1.0),
        'cache_v_a': nrm((DEC_BATCH, n_a, PAST_LEN, N_KV_A, HEAD_DIM), 1.0),
        'cache_k_b': nrm((DEC_BATCH, n_b, PAST_LEN, N_KV_GQA, HEAD_DIM), 1.0),
        'cache_v_b': nrm((DEC_BATCH, n_b, PAST_LEN, N_KV_GQA, HEAD_DIM), 1.0),
        'cache_k_c': nrm((DEC_BATCH, n_c, PAST_LEN, N_KV_GQA, HEAD_DIM), 1.0),
        'cache_v_c': nrm((DEC_BATCH, n_c, PAST_LEN, N_KV_GQA, HEAD_DIM), 1.0),
        'c': nrm((DEC_BATCH, D_MODEL), 1.0),
        'c_ctx': nrm((D_MODEL,), 1.0),
        'ada_w': nrm((DEPTH, D_MODEL, ADA_CHUNKS * D_MODEL), 0.5 * fan),
        'ada_b': nrm((DEPTH, ADA_CHUNKS * D_MODEL), 0.01),
        'norm_mix_g': 1.0 + nrm((DEPTH, D_MODEL), 0.01),
        'norm_mlp_g': 1.0 + nrm((DEPTH, D_MODEL), 0.01),
        'w_o': nrm((DEPTH, D_MODEL, D_MODEL), fan),
        'mlp_w1': nrm((DEPTH, D_MODEL, D_FF), fan),
        'mlp_b1': nrm((DEPTH, D_FF), 0.01),
        'mlp_w2': nrm((DEPTH, D_FF, D_MODEL), D_FF ** -0.5),
        'mlp_b2': nrm((DEPTH, D_MODEL), 0.01),
        'w_qkv_a': nrm((n_a, D_MODEL, qkv_a), fan),
        'rpb_a': nrm((n_a, N_HEADS, 2 * WIN_H - 1, 2 * WIN_W - 1), 0.1),
        'w_qkv_b': nrm((n_b, D_MODEL, qkv_g), fan),
        'sink_b': nrm((n_b, N_HEADS), 0.5),
        'w_qkv_c': nrm((n_c, D_MODEL, qkv_g), fan),
        'q_norm_c': 1.0 + nrm((n_c, HEAD_DIM), 0.01),
        'k_norm_c': 1.0 + nrm((n_c, HEAD_DIM), 0.01),
        'final_norm_g': 1.0 + nrm((D_MODEL,), 0.01),
    }


def reference(x_prompt, x_sample, cache_k_a, cache_v_a, cache_k_b, cache_v_b, cache_k_c, cache_v_c,
              c, c_ctx, ada_w, ada_b, norm_mix_g, norm_mlp_g, w_o, mlp_w1, mlp_b1, mlp_w2, mlp_b2,
              w_qkv_a, rpb_a, w_qkv_b, sink_b, w_qkv_c, q_norm_c, k_norm_c, final_norm_g):
    w_qkv = (w_qkv_a, w_qkv_b, w_qkv_c)
    n_kv = (N_KV_A, N_KV_GQA, N_KV_GQA)
    caches_k = (cache_k_a, cache_k_b, cache_k_c)
    caches_v = (cache_v_a, cache_v_b, cache_v_c)
    new_k = ([], [], [])
    new_v = ([], [], [])
    shared = (ada_w, ada_b, norm_mix_g, norm_mlp_g, w_o, mlp_w1, mlp_b1, mlp_w2, mlp_b2)
    xp, xs = x_prompt, x_sample
    cond_ctx = c_ctx[None, :]

    for l in range(DEPTH):
        m, j = l % N_MIXERS, l // N_MIXERS
        sink = sink_b[j].reshape(N_KV_GQA, N_HEADS // N_KV_GQA) if m == 1 else None

        def ctx_mix(h):
            q, k, v = split_qkv(h @ w_qkv[m][j], n_kv[m])
            if m == 2:
                q, k = rmsnorm(q, q_norm_c[j]), rmsnorm(k, k_norm_c[j])
            return dense_blocked(q, k, v, sink=sink), (k, v)

        xp, (k_new, v_new) = sublayers(xp, cond_ctx, l, ctx_mix, *shared)
        new_k[m].append(k_new)
        new_v[m].append(v_new)

        def lat_mix(h):
            q, k, v = split_qkv(h @ w_qkv[m][j], n_kv[m])
            k_ctx, v_ctx = caches_k[m][:, j], caches_v[m][:, j]
            if m == 0:
                o = neighbourhood_attention(q[:, :, :, 0], k, v, k_ctx, v_ctx, rpb_a[j])
            elif m == 1:
                o = windowed_sink_attention(rope_2d(q), rope_2d(k), v, k_ctx, v_ctx, sink)
            else:
                q = rope_2d(rmsnorm(q, q_norm_c[j]))
                k = rope_2d(rmsnorm(k, k_norm_c[j]))
                o = dense_blocked(q, jnp.concatenate([k, k_ctx], axis=1), jnp.concatenate([v, v_ctx], axis=1))
            return o, None

        xs, _ = sublayers(xs, c, l, lat_mix, *shared)

    y_prompt = rmsnorm(xp, final_norm_g)
    y_sample = rmsnorm(xs, final_norm_g)
    k_a = jnp.stack(new_k[0], axis=1)
    v_a = jnp.stack(new_v[0], axis=1)
    k_b = jnp.stack(new_k[1], axis=1)
    v_b = jnp.stack(new_v[1], axis=1)
    k_c = jnp.stack(new_k[2], axis=1)
    v_c = jnp.stack(new_v[2], axis=1)
    return (y_prompt, y_sample, k_a, v_a, k_b, v_b, k_c, v_c)
```

```python
from contextlib import ExitStack
import os
import numpy as np
import concourse.bass as bass
import concourse.mybir as mybir
from concourse.bass_utils import run_bass_kernel_spmd

F32 = mybir.dt.float32
BF16 = mybir.dt.bfloat16
ALU = mybir.AluOpType
AF = mybir.ActivationFunctionType

D = 1024
NCH = 8
DFF = 4096
NEG = -30000.0
EPS = 1e-6
NCORES = 8


class Dep:
    __slots__ = ("w", "r", "excl")

    def __init__(self, excl=False):
        self.w = None
        self.r = {}
        self.excl = excl


class Prog:
    ENGS = ("pe", "act", "dve", "pool", "sp")
    DMA_RING = 8

    def __init__(self, nc, es):
        self.nc = nc
        self.ops = {e: [] for e in self.ENGS}
        self.cnt = {e: 0 for e in self.ENGS}
        self.known = {e: {} for e in self.ENGS}
        self.sems = {}
        for e in self.ENGS:
            self.sems[e] = es.enter_context(nc.semaphore("s_" + e))
        self.dring, self.dval, self.dpos = {}, {}, {}
        for q in ("sp", "act", "pool"):
            self.dring[q] = []
            for i in range(self.DMA_RING):
                k = "d_%s_%d" % (q, i)
                self.sems[k] = es.enter_context(nc.semaphore(k))
                self.dval[k] = 0
                self.dring[q].append(k)
            self.dpos[q] = 0
        self.n_inst = 0

    def _need(self, eng, ev, needs):
        if ev is None:
            return
        k, v = ev
        if eng == "pe" and k == "pe":
            return
        if self.known[eng].get(k, 0) >= v:
            return
        if needs.get(k, 0) < v:
            needs[k] = v

    def _collect(self, eng, reads, writes, is_dma=False):
        needs = {}
        for d in reads:
            self._need(eng, d.w, needs)
            if d.excl:
                for k, v in d.r.items():
                    if k != eng:
                        self._need(eng, (k, v), needs)
        for d in writes:
            self._need(eng, d.w, needs)
            for k, v in d.r.items():
                self._need(eng, (k, v), needs)
        return needs

    def _emit_waits(self, eng, needs):
        for k, v in needs.items():
            if k in self.cnt:
                assert self.cnt[k] >= v, "wait on unsignaled event %s %d>%d" % (k, v, self.cnt[k])
            sem = self.sems[k]
            self.ops[eng].append(lambda e, sem=sem, v=v: e.wait_ge(sem, v))
            self.known[eng][k] = v
            self.n_inst += 1

    def op(self, eng, name, reads=(), writes=(), signal=True, **kw):
        fn = lambda e, name=name, kw=kw: getattr(e, name)(**kw)
        needs = self._collect(eng, reads, writes)
        self._emit_waits(eng, needs)
        val = self.cnt[eng] + 1
        for d in reads:
            d.r[eng] = val
        for d in writes:
            d.w = (eng, val)
            d.r = {}
        if signal:
            self.cnt[eng] = val
            sem = self.sems[eng]
            self.ops[eng].append(lambda e, fn=fn, sem=sem: fn(e).then_inc(sem, 1))
        else:
            self.ops[eng].append(lambda e, fn=fn: fn(e))
        self.n_inst += 1

    def dma(self, q, out, in_, reads=(), writes=(), **kw):
        needs = self._collect(q, reads, writes, is_dma=True)
        k = self.dring[q][self.dpos[q]]
        self.dpos[q] = (self.dpos[q] + 1) % len(self.dring[q])
        if self.dval[k] > 0:
            self._need(q, (k, self.dval[k]), needs)
        self._emit_waits(q, needs)
        self.dval[k] += 16
        v = self.dval[k]
        for d in reads:
            d.r[k] = v
        for d in writes:
            d.w = (k, v)
            d.r = {}
        sem = self.sems[k]
        self.ops[q].append(
            lambda e, out=out, in_=in_, sem=sem, kw=kw: e.dma_start(out=out, in_=in_, **kw).then_inc(sem, 16))
        self.n_inst += 1

    def barrier(self):
        for e in self.ENGS:
            needs = {}
            for k in self.ENGS:
                if k != e and self.cnt[k] > 0:
                    self._need(e, (k, self.cnt[k]), needs)
            for k, v in self.dval.items():
                if v > 0:
                    self._need(e, (k, v), needs)
            self._emit_waits(e, needs)

    def run(self):
        self.barrier()
        with self.nc.Block() as block:
            ops = self.ops

            @block.tensor
            def _(e):
                for f in ops["pe"]:
                    f(e)

            @block.scalar
            def _(e):
                for f in ops["act"]:
                    f(e)

            @block.vector
            def _(e):
                for f in ops["dve"]:
                    f(e)

            @block.gpsimd
            def _(e):
                for f in ops["pool"]:
                    f(e)

            @block.sync
            def _(e):
                for f in ops["sp"]:
                    f(e)


class Ring:
    def __init__(self, items):
        self.items = items
        self.i = 0

    def next(self):
        it = self.items[self.i]
        self.i = (self.i + 1) % len(self.items)
        return it


class Prefetch:
    def __init__(self, ring, issue, items, depth):
        self.ring, self.issue, self.items, self.depth = ring, issue, items, depth
        self.nxt = 0
        self.slots = {}

    def get(self, i):
        while self.nxt < min(len(self.items), i + self.depth + 1):
            tile, dep = self.ring.next()
            self.issue(tile, dep, self.items[self.nxt])
            self.slots[self.nxt] = (tile, dep)
            self.nxt += 1
        return self.slots.pop(i)

    def prime(self):
        self.slots[-1] = None
        self.get(-1)


def build_program(n_layers=4, passes=("P", "S")):
    nc = bass.Bass("TRN2", target_bir_lowering=False)

    def din(name, shape, dt=F32):
        return nc.dram_tensor(name, list(shape), dt, kind="ExternalInput").ap()

    def dout(name, shape):
        return nc.dram_tensor(name, list(shape), F32, kind="ExternalOutput").ap()

    I = {}
    I["xp"] = din("xp", [1024, D])
    I["xs"] = din("xs", [2048, D])
    I["cka"] = din("cka", [2, 512, 1024]); I["cva"] = din("cva", [2, 512, 1024])
    I["ckb"] = din("ckb", [1, 512, 256]); I["cvb"] = din("cvb", [1, 512, 256])
    I["ckc"] = din("ckc", [1, 512, 256]); I["cvc"] = din("cvc", [1, 512, 256])
    I["cond"] = din("cond", [2, D])
    I["ada_w"] = din("ada_w", [4, D, 6 * D]); I["ada_b"] = din("ada_b", [4, 6 * D])
    I["gmix"] = din("gmix", [4, D]); I["gmlp"] = din("gmlp", [4, D])
    I["w_o"] = din("w_o", [4, D, D])
    I["w1"] = din("w1", [4, D, DFF]); I["b1"] = din("b1", [4, DFF])
    I["w2"] = din("w2", [4, DFF, D]); I["b2"] = din("b2", [4, D])
    I["wqa"] = din("wqa", [2, D, 3072]); I["wqb"] = din("wqb", [1, D, 1536]); I["wqc"] = din("wqc", [1, D, 1536])
    I["sink"] = din("sink", [1, 16])
    I["qn"] = din("qn", [1, 64]); I["kn"] = din("kn", [1, 64])
    I["gfin"] = din("gfin", [D])
    I["rpbt"] = din("rpbt", [2, 16, 2, 128, 896])
    I["ident"] = din("ident", [128, 128])
    I["ropec"] = din("ropec", [128, 2048]); I["ropes"] = din("ropes", [128, 2048])
    I["bandm"] = din("bandm", [128, 384])
    O = {}
    O["yp"] = dout("yp", [1024, D]); O["ys"] = dout("ys", [2048, D])
    O["ka"] = dout("ka", [4, 2, 256, 1024]); O["va"] = dout("va", [4, 2, 256, 1024])
    O["kb"] = dout("kb", [4, 1, 256, 256]); O["vb"] = dout("vb", [4, 1, 256, 256])
    O["kc"] = dout("kc", [4, 1, 256, 256]); O["vc"] = dout("vc", [4, 1, 256, 256])

    with ExitStack() as es:
        P = Prog(nc, es)

        def sb(name, shape, dt):
            return es.enter_context(nc.sbuf_tensor("sb_" + name, list(shape), dt))

        xT = sb("xT", [128, NCH, 2048], F32)
        hT = sb("hT", [128, NCH, 2048], BF16)
        ropeC = sb("ropeC", [128, 2048], F32)
        ropeS = sb("ropeS", [128, 2048], F32)
        ident = sb("ident", [128, 128], F32)
        ones_bf = sb("ones_bf", [128, 512], BF16)
        bones_bf = sb("bones_bf", [128, 128], BF16)
        bandm = sb("bandm", [128, 384], BF16)
        condT = sb("condT", [128, NCH, 2], F32)
        sT = sb("sT", [128, NCH, 2], BF16)
        mods = sb("mods", [128, 4, 48, 2], F32)
        adabT = sb("adabT", [128, 4, 48], F32)
        gmixT = sb("gmixT", [128, 4, NCH], F32)
        gmlpT = sb("gmlpT", [128, 4, NCH], F32)
        b1T = sb("b1T", [128, 4, 32], F32)
        b2T = sb("b2T", [128, 4, NCH], F32)
        gfinT = sb("gfinT", [128, NCH], F32)
        gA = sb("gA", [128, 4, NCH, 2], F32)
        gFm = sb("gFm", [128, 4, NCH, 2], F32)
        b2g = sb("b2g", [128, 4, NCH, 2], F32)
        qn128 = sb("qn128", [128, 4], F32)
        knb = sb("knb", [128, 64], F32)
        es16 = sb("es16", [1, 16], F32)
        rstd = sb("rstd", [128, 512], F32)
        rsum_t = [sb("rsum%d" % i, [128, 512], F32) for i in range(2)]
        tmpf = [sb("tmpf%d" % i, [128, 512], F32) for i in range(3)]
        ptb = [sb("ptb%d" % i, [128, 512], BF16) for i in range(5)]
        ARENA_B = 56896
        arena = sb("arena", [128, ARENA_B // 2], BF16)

        d_const = Dep()
        d_mods = Dep()
        d_rstd = Dep()
        rsum_ring = Ring([(rsum_t[i], Dep()) for i in range(2)])
        tmpf_ring = Ring([(tmpf[i], Dep()) for i in range(3)])
        pt_ring = Ring([(ptb[i], Dep()) for i in range(5)])

        class Arena:
            def __init__(self):
                self.off = 0

            def reset(self):
                self.off = 0

            def alloc(self, shape, dt):
                n = int(np.prod(shape))
                nb = n * (4 if dt == F32 else 2)
                off = (self.off + 63) // 64 * 64
                assert off + nb <= ARENA_B, "arena overflow %d" % (off + nb)
                self.off = off + nb
                v = arena[:, off // 2: (off + nb) // 2]
                if dt == F32:
                    v = v.bitcast(F32)
                if len(shape) == 2:
                    v = v.rearrange("p (a b) -> p a b", a=shape[0])
                elif len(shape) == 3:
                    v = v.rearrange("p (a b c) -> p a b c", a=shape[0], b=shape[1])
                return v

        AR = Arena()

        psb = [es.enter_context(nc.psum_tensor("ps%d" % i, [128, 512], F32)) for i in range(8)]
        psd = [Dep(excl=True) for _ in range(8)]
        ps_s = Ring([(psb[i], psd[i]) for i in range(0, 4)])
        ps_o = Ring([(psb[i], psd[i]) for i in range(4, 6)])
        ps_m = Ring([(psb[i], psd[i]) for i in range(6, 8)])

        def mm(out, lhsT, rhs, start, stop, reads, writes, signal=None):
            if signal is None:
                signal = stop
            P.op("pe", "matmul", reads=reads, writes=writes, signal=signal, out=out, lhsT=lhsT, rhs=rhs,
                 start=start, stop=stop, skip_group_check=True)

        def tr(out, in_, reads, writes, signal=True):
            P.op("pe", "transpose", reads=reads, writes=writes, signal=signal, out=out, in_=in_, identity=ident[:])

        def act(out, in_, func, reads, writes, bias=None, scale=None):
            kw = dict(out=out, in_=in_, func=func)
            if bias is not None:
                kw["bias"] = bias
            if scale is not None:
                kw["scale"] = scale
            P.op("act", "activation", reads=reads, writes=writes, **kw)

        def tt(eng, out, in0, in1, op, reads, writes):
            P.op(eng, "tensor_tensor", reads=reads, writes=writes, out=out, in0=in0, in1=in1, op=op)

        def stt(eng, out, in0, scalar, in1, op0, op1, reads, writes):
            P.op(eng, "scalar_tensor_tensor", reads=reads, writes=writes, out=out, in0=in0, scalar=scalar, in1=in1,
                 op0=op0, op1=op1)

        nck = dict(allow_slow_non_contiguous=True)
        P.dma("sp", ident[:], I["ident"])
        P.op("pool", "memset", ap=ones_bf[:], constant=1.0)
        d_bones, d_sink = Dep(), Dep()
        P.op("pool", "memset", writes=[d_bones], ap=bones_bf[:], constant=0.0)
        P.op("pool", "memset", writes=[d_bones], ap=bones_bf[0:64, 0:64], constant=1.0)
        P.op("pool", "memset", writes=[d_bones], ap=bones_bf[64:128, 64:128], constant=1.0)
        P.dma("pool", bandm[:], I["bandm"])
        P.dma("sp", ropeC[:], I["ropec"])
        P.dma("sp", ropeS[:], I["ropes"])
        for ci in range(2):
            P.dma("sp", condT[:, :, ci], I["cond"][ci].rearrange("(k p) -> p k", p=128), **nck)
        for l in range(4):
            P.dma("sp", adabT[:, l, :], I["ada_b"][l].rearrange("(c p) -> p c", p=128), **nck)
            P.dma("sp", gmixT[:, l, :], I["gmix"][l].rearrange("(k p) -> p k", p=128), **nck)
            P.dma("sp", gmlpT[:, l, :], I["gmlp"][l].rearrange("(k p) -> p k", p=128), **nck)
            P.dma("sp", b1T[:, l, :], I["b1"][l].rearrange("(c p) -> p c", p=128), **nck)
            P.dma("sp", b2T[:, l, :], I["b2"][l].rearrange("(k p) -> p k", p=128), **nck)
        P.dma("sp", gfinT[:], I["gfin"].rearrange("(k p) -> p k", p=128), **nck)
        for half in range(2):
            for ti, nm in ((0, "qn"), (2, "kn")):
                src = I[nm].rearrange("o d -> d o")
                P.dma("sp", qn128[half * 64:(half + 1) * 64, ti:ti + 1], src, **nck)
                for blk in range(4):
                    pb = blk ^ 1
                    P.dma("sp", qn128[half * 64 + blk * 16: half * 64 + blk * 16 + 16, ti + 1:ti + 2],
                          src[pb * 16:(pb + 1) * 16, :], **nck)
        P.dma("sp", knb[:], I["kn"].to_broadcast([128, 64]), **nck)
        P.dma("sp", es16[:], I["sink"])
        P.barrier()
        act(es16[:], es16[:], AF.Exp, reads=[d_const], writes=[d_const])

        STOP = int(os.environ.get("MK_STOP", "99"))
        DBG = int(os.environ.get("MK_DBG", "0"))
        sg, sgd = tmpf_ring.next()
        condF = condT[:].rearrange("p k c -> p (k c)")
        act(sg[:, 0:16], condF, AF.Exp, reads=[], writes=[sgd], scale=-1.0)
        P.op("dve", "tensor_scalar_add", reads=[sgd], writes=[sgd], out=sg[:, 0:16], in0=sg[:, 0:16], scalar1=1.0)
        P.op("dve", "reciprocal", reads=[sgd], writes=[sgd], out=sg[:, 0:16], in_=sg[:, 0:16])
        tt("dve", sT[:].rearrange("p k c -> p (k c)"), sg[:, 0:16], condF, ALU.mult, reads=[sgd], writes=[d_mods])
        d_sT = d_mods
        d_modl = [Dep() for _ in range(4)]

        def ada_alloc(l):
            ring = Ring([(AR.alloc([NCH, 768], BF16), Dep()) for _ in range(2)])
            pf = Prefetch(ring, lambda wt, wd, p_: P.dma(
                "pool", wt, I["ada_w"][l][:, p_ * 768:(p_ + 1) * 768].rearrange("(k p) n -> p k n", p=128),
                writes=[wd]), list(range(8)), 1)
            pf.prime()
            return pf

        def ada_piece(l, pf, piece):
            wt, wd = pf.get(piece)
            pt, pdp = ps_m.next()
            for cc in range(6):
                for k in range(NCH):
                    mm(pt[:, cc * 2: cc * 2 + 2], wt[:, k, cc * 128:(cc + 1) * 128], sT[:, k, :],
                       start=(k == 0), stop=(k == NCH - 1), reads=[wd, d_sT], writes=[pdp])
            tt("dve", mods[:, l, piece * 6:(piece + 1) * 6, :], pt[:, 0:12].rearrange("p (c t) -> p c t", t=2),
               adabT[:, l, piece * 6:(piece + 1) * 6].unsqueeze(2).to_broadcast([128, 6, 2]), ALU.add, reads=[pdp],
               writes=[d_modl[l]])
            if piece == 7:
                stt("dve", gA[:, l, :, :], mods[:, l, 8:16, :], 1.0,
                    gmixT[:, l, :].unsqueeze(2).to_broadcast([128, NCH, 2]), ALU.add, ALU.mult, reads=[d_modl[l]],
                    writes=[d_modl[l]])
                stt("dve", gFm[:, l, :, :], mods[:, l, 32:40, :], 1.0,
                    gmlpT[:, l, :].unsqueeze(2).to_broadcast([128, NCH, 2]), ALU.add, ALU.mult, reads=[d_modl[l]],
                    writes=[d_modl[l]])
                tt("dve", b2g[:, l, :, :], mods[:, l, 40:48, :], b2T[:, l, :].unsqueeze(2).to_broadcast([128, NCH, 2]),
                   ALU.mult, reads=[d_modl[l]], writes=[d_modl[l]])

        lazy_ada = passes[0] == "P"
        AR.reset()
        for l in range(0 if lazy_ada else n_layers):
            pf0 = ada_alloc(l)
            for piece in range(8):
                ada_piece(l, pf0, piece)
            P.barrier()
            AR.reset()
        P.barrier()

        def ckpt(n):
            if STOP <= n:
                raise StopIteration

        def rms_stats(srcs, tcols, ones_t, scale, out_t, out_d, src_psum=False):
            pt, pdp = ps_m.next()
            nk = len(srcs)
            for k, (src, sdeps) in enumerate(srcs):
                sq, sqd = pt_ring.next()
                if k % 2 == 0 or src_psum:
                    act(sq[:, 0:tcols], src, AF.Square, reads=sdeps, writes=[sqd])
                else:
                    tt("dve", sq[:, 0:tcols], src, src, ALU.mult, reads=sdeps, writes=[sqd])
                mm(pt[:, 0:tcols], ones_t, sq[:, 0:tcols], start=(k == 0), stop=(k == nk - 1),
                   reads=[sqd], writes=[pdp], signal=True)
            act(out_t[:, 0:tcols], pt[:, 0:tcols], AF.Ln, reads=[pdp], writes=[out_d], bias=EPS, scale=scale)
            act(out_t[:, 0:tcols], out_t[:, 0:tcols], AF.Exp, reads=[out_d], writes=[out_d], scale=-0.5)

        def run_pass(grp):
            ckpt(1)
            sample = grp == "S"
            NT = 2048 if sample else 1024
            NTC = NT // 512
            NTT = NT // 128
            ci = 1 if sample else 0
            xin = I["xs"] if sample else I["xp"]
            yout = O["ys"] if sample else O["yp"]
            xd = [[Dep() for _ in range(NTC)] for _ in range(NCH)]
            hd = [[Dep() for _ in range(NCH)] for _ in range(NTC)]

            AR.reset()
            stage = [AR.alloc([1024], F32) for _ in range(2)]
            st_ring = Ring([(stage[i], Dep()) for i in range(2)])
            pf_l0 = None
            if lazy_ada and grp == passes[0]:
                pf_l0 = ada_alloc(0)
            for t_ in range(NTT):
                stg, sd = st_ring.next()
                P.dma("sp", stg, xin[t_ * 128:(t_ + 1) * 128, :], writes=[sd])
                for half in range(2):
                    pt, pdp = ps_s.next()
                    for kk in range(4):
                        k = half * 4 + kk
                        tr(pt[:, kk * 128:(kk + 1) * 128], stg[:, k * 128:(k + 1) * 128], reads=[sd], writes=[pdp],
                           signal=(kk == 3))
                    tc = t_ // 4
                    dst = xT[:, half * 4:(half + 1) * 4, t_ * 128:(t_ + 1) * 128]
                    src = pt[:, :].rearrange("p (k t) -> p k t", k=4)
                    wr = [xd[half * 4 + kk][tc] for kk in range(4)]
                    if half == 0:
                        P.op("act", "copy", reads=[pdp], writes=wr, out=dst, in_=src)
                    else:
                        P.op("dve", "tensor_copy", reads=[pdp], writes=wr, out=dst, in_=src)
            if pf_l0 is not None:
                for piece in range(8):
                    ada_piece(0, pf_l0, piece)
            P.barrier()
            ckpt(2)

            def norm_to_h(tc, gs, sh):
                cols = slice(tc * 512, (tc + 1) * 512)
                rms_stats([(xT[:, k, cols], [xd[k][tc]]) for k in range(NCH)], 512, ones_bf[:, 0:128], 1.0 / D,
                          rstd, d_rstd)
                for k in range(NCH):
                    t, td = tmpf_ring.next()
                    tt("dve", t[:], xT[:, k, cols], rstd[:], ALU.mult, reads=[xd[k][tc], d_rstd], writes=[td])
                    act(hT[:, k, cols], t[:], AF.Identity, reads=[td], writes=[hd[tc][k]], bias=sh[:, k:k + 1],
                        scale=gs[:, k:k + 1])

            for l in range(n_layers):
                m, j = l % 3, l // 3
                wqkv = (I["wqa"], I["wqb"], I["wqc"])[m][j]
                gqa = m != 0
                rope = sample and m != 0
                qknorm = m == 2
                k_out = (O["ka"], O["kb"], O["kc"])[m]
                v_out = (O["va"], O["vb"], O["vc"])[m]
                ck = (I["cka"], I["ckb"], I["ckc"])[m][j]
                cv = (I["cva"], I["cvb"], I["cvc"])[m][j]

                ckpt(3)
                AR.reset()
                NK = NT + (512 if sample else 0)
                NKT = NK // 128
                QTz = AR.alloc([2, NT], BF16)
                d_qz = Dep()
                P.op("pool", "memset", writes=[d_qz], ap=QTz[:, :, :], constant=0.0)
                sinkrow = None
                if m == 1:
                    sinkrow = AR.alloc([16, 128], BF16)
                    d_sk = Dep()
                    P.op("dve", "memset", writes=[d_sk], ap=sinkrow[0:1, :, :], constant=0.0)
                    for h_ in range(16):
                        lo = 64 if h_ % 2 == 0 else 0
                        P.op("dve", "tensor_copy", reads=[], writes=[d_sk], out=sinkrow[0:1, h_, lo:lo + 64],
                             in_=es16[0:1, h_:h_ + 1].to_broadcast([1, 64]))
                KT = AR.alloc([NK], BF16)
                VA = AR.alloc([NKT, 192], BF16)
                oT = AR.alloc([NT], BF16)
                wq = AR.alloc([NCH, 128], BF16); wk = AR.alloc([NCH, 128], BF16); wv = AR.alloc([NCH, 128], BF16)
                wo = AR.alloc([1024], BF16)
                wqr = wkr = None
                if rope:
                    wqr = AR.alloc([NCH, 128], BF16); wkr = AR.alloc([NCH, 128], BF16)
                kst = AR.alloc([4, 128], F32)
                d_kst = Dep()
                ko_ring = vo_ring = bias_ring = None
                if not sample:
                    ko_ring = Ring([(AR.alloc([128], F32), Dep()) for _ in range(3)])
                    vo_ring = Ring([(AR.alloc([128], F32), Dep()) for _ in range(3)])
                if sample and m == 0:
                    bias_ring = Ring([(AR.alloc([2, 896], F32), Dep()) for _ in range(3)])
                d_wq, d_wk, d_wv, d_wo, d_wqr, d_wkr = [Dep() for _ in range(6)]
                ada_next = None
                if lazy_ada and grp == passes[0] and l + 1 < n_layers:
                    ada_next = ada_alloc(l + 1)
                qd = [[Dep(), Dep()] for _ in range(NTC)]
                kd = [Dep() for _ in range(NTC + 1)]
                vd = [[Dep(), Dep()] for _ in range(NKT)]
                odmap = {}
                pending_wo = {tc_: [] for tc_ in range(NTC)}
                wo_state = {"next": None}

                def pop_wo(tc_only=None):
                    if tc_only is not None:
                        items = pending_wo[tc_only]
                        pending_wo[tc_only] = []
                        for it in items:
                            it()
                    else:
                        for tc_ in range(NTC):
                            if pending_wo[tc_]:
                                pending_wo[tc_].pop(0)()
                                break
                    if wo_state["next"] is not None and not any(pending_wo.values()):
                        load_wo(wo_state["next"])
                        wo_state["next"] = None
                d_ones = Dep()
                P.op("pool", "memset", writes=[d_ones], ap=VA[:, :, 64:128], constant=1.0)
                wsrc = wqkv.rearrange("(k p) n -> p k n", p=128)

                def load_qkv(c):
                    g = c // 2
                    P.dma("pool", wq, wsrc[:, :, c * 128:(c + 1) * 128], writes=[d_wq])
                    if gqa and c % 2 == 1:
                        return
                    if gqa:
                        for hf in range(2):
                            P.dma("pool", wk[:, :, hf * 64:(hf + 1) * 64], wsrc[:, :, 1024 + g * 64:1024 + (g + 1) * 64],
                                  writes=[d_wk])
                            P.dma("pool", wv[:, :, hf * 64:(hf + 1) * 64], wsrc[:, :, 1280 + g * 64:1280 + (g + 1) * 64],
                                  writes=[d_wv])
                    else:
                        P.dma("pool", wk, wsrc[:, :, 1024 + c * 128:1024 + (c + 1) * 128], writes=[d_wk])
                        P.dma("pool", wv, wsrc[:, :, 2048 + c * 128:2048 + (c + 1) * 128], writes=[d_wv])

                def load_wo(c):
                    P.dma("pool", wo, I["w_o"][l][c * 128:(c + 1) * 128, :], writes=[d_wo])

                def load_kst(c):
                    g = c // 2
                    if gqa:
                        for hf in range(2):
                            P.dma("sp", kst[:, :, hf * 64:(hf + 1) * 64],
                                  ck[:, g * 64:(g + 1) * 64].rearrange("(t p) d -> p t d", p=128), writes=[d_kst])
                    else:
                        P.dma("sp", kst[:, :, :], ck[:, c * 128:(c + 1) * 128].rearrange("(t p) d -> p t d", p=128),
                              writes=[d_kst])

                bias_pf = None
                if sample and m == 0:
                    bias_pf = Prefetch(bias_ring, lambda bt, btd, h: P.dma(
                        "sp", bt, I["rpbt"][j, h].rearrange("k p n -> p k n"), writes=[btd]), list(range(16)), 1)
                    bias_pf.prime()
                load_qkv(0)
                load_wo(0)
                if sample:
                    load_kst(0)

                for tc in range(NTC):
                    norm_to_h(tc, gA[:, l, :, ci], mods[:, l, 0:8, ci])

                for c in range(8):
                    g = c // 2
                    if rope:
                        for (src_t, dst_t, sdp, ddp) in (((wq, wqr, d_wq, d_wqr), (wk, wkr, d_wk, d_wkr))
                                                         if ((not gqa) or c % 2 == 0) else ((wq, wqr, d_wq, d_wqr),)):
                            sv = src_t.rearrange("p k (a b r) -> p k a b r", a=4, b=2)
                            dv = dst_t.rearrange("p k (a b r) -> p k a b r", a=4, b=2)
                            for b_ in range(2):
                                P.op("dve", "tensor_copy", reads=[sdp], writes=[ddp], out=dv[:, :, :, b_, :],
                                     in_=sv[:, :, :, 1 - b_, :])

                    newkv = (not gqa) or (c % 2 == 0)

                    def proj_T(dst_fn, dstd, w_t, wd_, wr_t, wrd_, gcol, split):
                        for tc in range(NTC):
                            cols = slice(tc * 512, (tc + 1) * 512)
                            pq, pqd = ps_s.next()
                            for k in range(NCH):
                                mm(pq[:], w_t[:, k, :], hT[:, k, cols], start=(k == 0), stop=(k == NCH - 1),
                                   reads=[wd_, hd[tc][k]], writes=[pqd])
                            if rope:
                                pr, prd = ps_s.next()
                                for k in range(NCH):
                                    mm(pr[:], wr_t[:, k, :], hT[:, k, cols], start=(k == 0), stop=(k == NCH - 1),
                                       reads=[wrd_, hd[tc][k]], writes=[prd])
                            if qknorm:
                                rms_stats([(pq[:], [pqd])], 512, bones_bf[:], 1.0 / 64, rstd, d_rstd, src_psum=True)
                            halves = [(0, slice(0, 64)), (1, slice(64, 128))] if split else [(None, slice(0, 128))]

                            def wdep(hh_):
                                return [dstd[tc][hh_]] if split else [dstd[tc]]
                            if rope:
                                t1, t1d = tmpf_ring.next()
                                t2, t2d = tmpf_ring.next()
                                if qknorm:
                                    stt("dve", t1[:], pq[:], qn128[:, gcol:gcol + 1], ropeC[:, cols], ALU.mult, ALU.mult,
                                        reads=[pqd], writes=[t1d])
                                    stt("dve", t2[:], pr[:], qn128[:, gcol + 1:gcol + 2], ropeS[:, cols], ALU.mult,
                                        ALU.mult, reads=[prd], writes=[t2d])
                                    tt("pool", t1[:], t1[:], t2[:], ALU.add, reads=[t1d, t2d], writes=[t1d])
                                    for hh_, prt in halves:
                                        tt("pool", dst_fn(hh_, prt, cols), t1[prt, :], rstd[prt, :], ALU.mult,
                                           reads=[t1d, d_rstd], writes=wdep(hh_))
                                else:
                                    tt("dve", t1[:], pq[:], ropeC[:, cols], ALU.mult, reads=[pqd], writes=[t1d])
                                    tt("dve", t2[:], pr[:], ropeS[:, cols], ALU.mult, reads=[prd], writes=[t2d])
                                    for hh_, prt in halves:
                                        tt("pool", dst_fn(hh_, prt, cols), t1[prt, :], t2[prt, :], ALU.add,
                                           reads=[t1d, t2d], writes=wdep(hh_))
                            elif qknorm:
                                for hh_, prt in halves:
                                    stt("dve", dst_fn(hh_, prt, cols), pq[prt, :], qn128[prt, gcol:gcol + 1], rstd[prt, :],
                                        ALU.mult, ALU.mult, reads=[pqd, d_rstd], writes=wdep(hh_))
                            else:
                                for hh_, prt in halves:
                                    P.op("act", "copy", reads=[pqd], writes=wdep(hh_), out=dst_fn(hh_, prt, cols),
                                         in_=pq[prt, :])

                    proj_T(lambda hh_, prt, cols: QTz[prt, hh_, cols], qd, wq, d_wq, wqr, d_wqr, 0, True)
                    if newkv:
                        proj_T(lambda hh_, prt, cols: KT[prt, cols], kd, wk, d_wk, wkr, d_wkr, 2, False)
                    ckpt(4)

                    for t_ in (range(NTT) if newkv else ()):
                        tc = t_ // 4
                        tcols = slice(t_ * 128, (t_ + 1) * 128)
                        pv, pvd = ps_s.next()
                        for k in range(NCH):
                            mm(pv[:, 0:128], hT[:, k, tcols], wv[:, k, :], start=(k == 0), stop=(k == NCH - 1),
                               reads=[d_wv, hd[tc][k]], writes=[pvd])
                        do_out = (not sample) and ((not gqa) or c % 2 == 0)
                        if do_out:
                            for k in range(NCH):
                                mm(pv[:, 128:256], hT[:, k, tcols], wk[:, k, :], start=(k == 0), stop=(k == NCH - 1),
                                   reads=[d_wk, hd[tc][k]], writes=[pvd])
                        P.op("act", "copy", reads=[pvd], writes=[vd[t_][0]], out=VA[:, t_, 0:64], in_=pv[:, 0:64])
                        P.op("act", "copy", reads=[pvd], writes=[vd[t_][1]], out=VA[:, t_, 128:192], in_=pv[:, 64:128])
                        if do_out:
                            s_, r0 = t_ // 2, (t_ % 2) * 128
                            ncol = 64 if gqa else 128
                            c0 = g * 64 if gqa else c * 128
                            vo, vod = vo_ring.next()
                            P.op("dve", "tensor_copy", reads=[pvd], writes=[vod], out=vo[:, 0:ncol], in_=pv[:, 0:ncol])
                            if not (DBG & 1):
                                P.dma("sp", v_out[s_, j, r0:r0 + 128, c0:c0 + ncol], vo[:, 0:ncol], reads=[vod])
                            ko, kod = ko_ring.next()
                            if qknorm:
                                t, td = tmpf_ring.next()
                                P.op("dve", "tensor_copy", reads=[pvd], writes=[kod], out=ko[:, 0:64], in_=pv[:, 128:192])
                                tt("dve", t[:, 0:64], ko[:, 0:64], ko[:, 0:64], ALU.mult, reads=[kod], writes=[td])
                                P.op("dve", "tensor_reduce", reads=[td], writes=[td], out=t[:, 64:65], in_=t[:, 0:64],
                                     axis=mybir.AxisListType.X, op=ALU.add)
                                act(t[:, 64:65], t[:, 64:65], AF.Ln, reads=[td], writes=[td], bias=EPS, scale=1.0 / 64)
                                act(t[:, 64:65], t[:, 64:65], AF.Exp, reads=[td], writes=[td], scale=-0.5)
                                stt("dve", ko[:, 0:64], ko[:, 0:64], t[:, 64:65], knb[:], ALU.mult, ALU.mult,
                                    reads=[td, kod], writes=[kod])
                            else:
                                P.op("dve", "tensor_copy", reads=[pvd], writes=[kod], out=ko[:, 0:ncol],
                                     in_=pv[:, 128:128 + ncol])
                            if not (DBG & 1):
                                P.dma("sp", k_out[s_, j, r0:r0 + 128, c0:c0 + ncol], ko[:, 0:ncol], reads=[kod])

                    if c < 7:
                        load_qkv(c + 1)
                    if sample and newkv:
                        pk, pkd = ps_s.next()
                        for t4 in range(4):
                            tr(pk[:, t4 * 128:(t4 + 1) * 128], kst[:, t4, :], reads=[d_kst], writes=[pkd], signal=(t4 == 3))
                        P.op("act", "copy", reads=[pkd], writes=[kd[NTC]], out=KT[:, NT:NT + 512], in_=pk[:])
                        for hf in range(2):
                            csrc = cv[:, g * 64:(g + 1) * 64] if gqa else cv[:, c * 128 + hf * 64: c * 128 + hf * 64 + 64]
                            P.dma("pool", VA[:, NTT:NTT + 4, hf * 128: hf * 128 + 64],
                                  csrc.rearrange("(t p) d -> p t d", p=128), writes=[vd[NTT + t4][hf] for t4 in range(4)])

                    if sample and c < 7 and ((not gqa) or (c + 1) % 2 == 0):
                        load_kst(c + 1)
                    ckpt(5)

                    groups = []

                    def attend(hh, qlo, qn_, kparts, sinkh, tcq):
                        groups.append((hh, qlo, qn_, kparts, sinkh, tcq))

                    def run_groups():
                        LA = 3
                        flat = [(gi, pi) for gi, g_ in enumerate(groups) for pi in range(len(g_[3]))]
                        pend = {}
                        gstate = {}
                        for idx in range(len(flat) + LA):
                            if idx < len(flat):
                                gi, pi = flat[idx]
                                hh, qlo, qn_, kparts, sinkh, tcq = groups[gi]
                                hp = slice(hh * 64, (hh + 1) * 64)
                                kc, qo, ql, mode, extra = kparts[pi]
                                ps_, psd_ = ps_s.next()
                                ktc = min(kc // 4, NTC)
                                mm(ps_[:, 0:ql], KT[:, kc * 128:(kc + 1) * 128], QTz[:, hh, qlo + qo: qlo + qo + ql],
                                   start=True, stop=True, reads=[kd[ktc], qd[tcq][hh], d_qz], writes=[psd_])
                                pt_, ptd_ = pt_ring.next()
                                if mode == "bias":
                                    tb, tbd = tmpf_ring.next()
                                    bt, btd, j0 = extra
                                    stt("dve", tb[:, 0:ql], ps_[:, 0:ql], 0.125, bt[:, j0 * 64: j0 * 64 + ql], ALU.mult,
                                        ALU.add, reads=[psd_, btd], writes=[tbd])
                                    act(pt_[:, 0:ql], tb[:, 0:ql], AF.Exp, reads=[tbd], writes=[ptd_])
                                else:
                                    act(pt_[:, 0:ql], ps_[:, 0:ql], AF.Exp, reads=[psd_], writes=[ptd_], scale=0.125)
                                    if mode == "band":
                                        mo = extra
                                        for sub in range(ql // 128):
                                            mcol = mo + sub * 128
                                            if 128 <= mcol < 256:
                                                continue
                                            tt("dve", pt_[:, sub * 128:(sub + 1) * 128], pt_[:, sub * 128:(sub + 1) * 128],
                                               bandm[:, mcol:mcol + 128], ALU.mult, reads=[ptd_], writes=[ptd_])
                                pend[idx] = (pt_, ptd_)
                                pop_wo()
                            if idx >= LA:
                                gi, pi = flat[idx - LA]
                                hh, qlo, qn_, kparts, sinkh, tcq = groups[gi]
                                hp = slice(hh * 64, (hh + 1) * 64)
                                sp_ = slice((1 - hh) * 64, (2 - hh) * 64)
                                kc, qo, ql, mode, extra = kparts[pi]
                                pt_, ptd_ = pend.pop(idx - LA)
                                if pi == 0:
                                    gstate[gi] = ps_o.next()
                                po, pod = gstate[gi]
                                n = len(kparts)
                                last = (pi == n - 1) and sinkh is None
                                mm(po[:, qo:qo + ql], VA[:, kc, hh * 64: hh * 64 + 128], pt_[:, 0:ql],
                                   start=(pi == 0), stop=last, reads=vd[kc] + [ptd_, d_ones], writes=[pod], signal=True)
                                if pi == n - 1:
                                    if sinkh is not None:
                                        mm(po[:, 0:qn_], sinkrow[0:1, sinkh, :], ones_bf[0:1, 0:qn_], start=False, stop=True,
                                           reads=[], writes=[pod])
                                    rsum, d_rsum = rsum_ring.next()
                                    act(rsum[sp_, 0:qn_], po[sp_, 0:qn_], AF.Ln, reads=[pod], writes=[d_rsum])
                                    act(rsum[sp_, 0:qn_], rsum[sp_, 0:qn_], AF.Exp, reads=[d_rsum], writes=[d_rsum],
                                        scale=-1.0)
                                    pop_wo(tc_only=tcq)
                                    odp = odmap.setdefault((tcq, hh, qlo), Dep())
                                    tt("dve", oT[hp, qlo:qlo + qn_], po[hp, 0:qn_], rsum[sp_, 0:qn_], ALU.mult,
                                       reads=[pod, d_rsum], writes=[odp])
                                    del gstate[gi]

                    for hh in range(2):
                        h = 2 * c + hh
                        sinkh = h if m == 1 else None
                        if not sample:
                            for s_ in range(4):
                                attend(hh, s_ * 256, 256,
                                       [(2 * s_, 0, 256, "plain", None), (2 * s_ + 1, 0, 256, "plain", None)], sinkh, s_ // 2)
                            continue
                        ctxp = [(NTT + t4, 0, 512, "plain", None) for t4 in range(4)]
                        if m == 0:
                            bt, btd = bias_pf.get(h)
                        for iq in range(4):
                            parts = list(ctxp)
                            if m == 2:
                                parts += [(kc, 0, 512, "plain", None) for kc in range(16)]
                            elif m == 1:
                                for mk in range(max(0, 4 * iq - 1), min(15, 4 * iq + 4) + 1):
                                    ulo, uhi = max(mk - 1, 4 * iq), min(mk + 1, 4 * iq + 3)
                                    parts.append((mk, (ulo - 4 * iq) * 128, (uhi - ulo + 1) * 128, "band",
                                                  (ulo - (mk - 1)) * 128))
                            else:
                                for mk in range(16):
                                    rlo = max(8 * iq, 4, 2 * mk - 3)
                                    rhi = min(8 * iq + 7, 28, 2 * mk + 5)
                                    if rlo <= rhi:
                                        parts.append((mk, (rlo - 8 * iq) * 64, (rhi - rlo + 1) * 64, "bias",
                                                      (bt[:, 0, :], btd, rlo - 2 * mk + 6)))
                                    if mk <= 3 and iq == 0:
                                        parts.append((mk, 0, 4 * 64, "bias", (bt[:, 1, :], btd, 0 - 2 * mk + 6)))
                                    if mk >= 12 and iq == 3:
                                        parts.append((mk, (29 - 24) * 64, 3 * 64, "bias",
                                                      (bt[:, 1, :], btd, 29 - 2 * mk + 6)))
                            attend(hh, iq * 512, 512, parts, sinkh, iq)

                    if ada_next is not None:
                        ada_piece(l + 1, ada_next, c)
                    run_groups()
                    ckpt(6)
                    def wo_item(tc, jo):
                        cols = slice(tc * 512, (tc + 1) * 512)
                        pw, pwd = ps_m.next()
                        mm(pw[:], wo[:, jo * 128:(jo + 1) * 128], oT[:, cols], start=True, stop=True,
                           reads=[d_wo] + [d_ for (t_k, _, _), d_ in odmap.items() if t_k == tc], writes=[pwd])
                        stt("dve", xT[:, jo, cols], pw[:], mods[:, l, 16 + jo, ci:ci + 1], xT[:, jo, cols], ALU.mult,
                            ALU.add, reads=[pwd, xd[jo][tc]], writes=[xd[jo][tc]])

                    for tc in range(NTC):
                        for jo in range(NCH):
                            pending_wo[tc].append(lambda tc=tc, jo=jo: wo_item(tc, jo))
                    wo_state["next"] = c + 1 if c < 7 else None
                    if c == 7:
                        for tc in range(NTC):
                            pop_wo(tc_only=tc)
                P.barrier()

                ckpt(7)
                AR.reset()
                h1T = AR.alloc([16, 1024], BF16)
                w1_ring = Ring([(AR.alloc([NCH, 128], BF16), Dep()) for _ in range(3)])
                w2_ring = Ring([(AR.alloc([16, 128], BF16), Dep()) for _ in range(2)])
                rl_ring = Ring([(AR.alloc([512], BF16), Dep()) for _ in range(3)])
                h1d = [[Dep() for _ in range(2)] for _ in range(16)]
                NB = NT // 1024
                w1_pf = Prefetch(w1_ring, lambda wt, wd, ff: P.dma(
                    "pool", wt, I["w1"][l][:, ff * 128:(ff + 1) * 128].rearrange("(k p) n -> p k n", p=128), writes=[wd]),
                    [hf_ * 16 + f_ for b_ in range(NB) for hf_ in range(2) for f_ in range(16)], 2)
                w2_pf = Prefetch(w2_ring, lambda wt, wd, it: P.dma(
                    "pool", wt, I["w2"][l][it[0] * 2048:(it[0] + 1) * 2048, it[1] * 128:(it[1] + 1) * 128]
                    .rearrange("(f p) n -> p f n", p=128), writes=[wd]),
                    [(hf_, jo_) for b_ in range(NB) for hf_ in range(2) for jo_ in range(NCH)], 1)
                w1_pf.prime()
                w2_pf.prime()
                for blk in range(NB):
                    for t2 in range(2):
                        norm_to_h(blk * 2 + t2, gFm[:, l, :, ci], mods[:, l, 24:32, ci])
                    for t2 in range(2):
                        tc = blk * 2 + t2
                        cols = slice(tc * 512, (tc + 1) * 512)
                        for jo in range(NCH):
                            P.op("dve", "tensor_scalar", reads=[xd[jo][tc]], writes=[xd[jo][tc]], out=xT[:, jo, cols],
                                 in0=xT[:, jo, cols], scalar1=b2g[:, l, jo, ci:ci + 1], scalar2=None, op0=ALU.add)
                    for half in range(2):
                        for f in range(16):
                            ff = half * 16 + f
                            wt, wd = w1_pf.get((blk * 2 + half) * 16 + f)
                            for t2 in range(2):
                                tc = blk * 2 + t2
                                cols = slice(tc * 512, (tc + 1) * 512)
                                ph, phd = ps_s.next()
                                for k in range(NCH):
                                    mm(ph[:], wt[:, k, :], hT[:, k, cols], start=(k == 0), stop=(k == NCH - 1),
                                       reads=[wd, hd[tc][k]], writes=[phd])
                                r_, rd_ = rl_ring.next()
                                act(r_[:], ph[:], AF.Relu, reads=[phd], writes=[rd_], bias=b1T[:, l, ff:ff + 1])
                                tt("dve", h1T[:, f, t2 * 512:(t2 + 1) * 512], r_[:], r_[:], ALU.mult, reads=[rd_],
                                   writes=[h1d[f][t2]])
                        for jo in range(NCH):
                            wt, wd = w2_pf.get((blk * 2 + half) * NCH + jo)
                            for t2 in range(2):
                                tc = blk * 2 + t2
                                cols = slice(tc * 512, (tc + 1) * 512)
                                py, pyd = ps_o.next()
                                for f in range(16):
                                    mm(py[:], wt[:, f, :], h1T[:, f, t2 * 512:(t2 + 1) * 512], start=(f == 0),
                                       stop=(f == 15), reads=[wd, h1d[f][t2]], writes=[pyd])
                                stt("dve", xT[:, jo, cols], py[:], mods[:, l, 40 + jo, ci:ci + 1], xT[:, jo, cols],
                                    ALU.mult, ALU.add, reads=[pyd, xd[jo][tc]], writes=[xd[jo][tc]])
                P.barrier()

            ckpt(8)
            AR.reset()
            os_ring = Ring([(AR.alloc([4, 1024], F32), [Dep() for _ in range(NCH)]) for _ in range(2)])
            for tc in range(NTC):
                cols = slice(tc * 512, (tc + 1) * 512)
                rms_stats([(xT[:, k, cols], [xd[k][tc]]) for k in range(NCH)], 512, ones_bf[:, 0:128], 1.0 / D,
                          rstd, d_rstd)
                osb, osd = os_ring.next()
                for k in range(NCH):
                    t, td = tmpf_ring.next()
                    stt("dve", t[:], xT[:, k, cols], gfinT[:, k:k + 1], rstd[:], ALU.mult, ALU.mult,
                        reads=[xd[k][tc], d_rstd], writes=[td])
                    pt, pdp = ps_s.next()
                    for t4 in range(4):
                        tr(pt[:, t4 * 128:(t4 + 1) * 128], t[:, t4 * 128:(t4 + 1) * 128], reads=[td], writes=[pdp],
                           signal=(t4 == 3))
                    P.op("act", "copy", reads=[pdp], writes=[osd[k]], out=osb[:, :, k * 128:(k + 1) * 128],
                         in_=pt[:, :].rearrange("p (a b) -> p a b", a=4))
                P.dma("sp", yout[tc * 512:(tc + 1) * 512, :].rearrange("(t p) n -> p t n", p=128), osb, reads=osd)
            P.barrier()

        try:
            for g_ in passes:
                run_pass(g_)
        except StopIteration:
            pass
        P.run()
        print("bass program: %d instructions" % P.n_inst, flush=True)
    return nc


def _rope_tables():
    t = np.arange(2048)
    rows = (t // 64).astype(np.float32)
    cols = (t % 64).astype(np.float32)
    freqs = np.exp(-np.log(np.float32(10000.0)) * np.arange(16, dtype=np.float32) / np.float32(16)).astype(np.float32)
    C = np.zeros((64, 2048), np.float32)
    S = np.zeros((64, 2048), np.float32)
    for d in range(64):
        pos = rows if d < 32 else cols
        i = d % 16
        ang = (pos * freqs[i]).astype(np.float32)
        C[d] = np.cos(ang)
        sgn = -1.0 if (d % 32) < 16 else 1.0
        S[d] = sgn * np.sin(ang)
    return np.concatenate([C, C], 0), np.concatenate([S, S], 0)


def _rpb_tables(rpb_a):
    krl = np.arange(2)[:, None, None, None]
    kc = np.arange(64)[None, :, None, None]
    jj = np.arange(14)[None, None, :, None]
    qc = np.arange(64)[None, None, None, :]
    a = krl - jj + 13 + 0 * kc + 0 * qc
    b = kc - qc + 15 + 0 * krl + 0 * jj
    c0 = np.clip(qc - 8, 0, 48)
    vcol = (kc >= c0) & (kc < c0 + 16) & (b >= 0) & (b <= 30)
    vcol = np.broadcast_to(vcol, a.shape)
    ac = np.clip(a, 0, 14)
    bc = np.clip(b, 0, 30)
    out = np.empty((2, 16, 2, 128, 14 * 64), np.float32)
    for kind in range(2):
        vrow = ((a >= 3) & (a <= 10)) if kind == 0 else ((a >= 0) & (a <= 14))
        valid = (vrow & vcol).reshape(128, 896)
        g = rpb_a[:, :, ac, bc].reshape(2, 16, 128, 896)
        out[:, :, kind] = np.where(valid[None, None], g, np.float32(NEG))
    return out


_NC_CACHE = {}


def kernel(x_prompt, x_sample, cache_k_a, cache_v_a, cache_k_b, cache_v_b, cache_k_c, cache_v_c,
           c, c_ctx, ada_w, ada_b, norm_mix_g, norm_mlp_g, w_o, mlp_w1, mlp_b1, mlp_w2, mlp_b2,
           w_qkv_a, rpb_a, w_qkv_b, sink_b, w_qkv_c, q_norm_c, k_norm_c, final_norm_g):
    f = lambda a: np.ascontiguousarray(np.asarray(a, dtype=np.float32))
    n_layers = int(os.environ.get("MK_LAYERS", "4"))
    passes = tuple(os.environ.get("MK_PASSES", "PS"))
    key = (n_layers, passes)
    if key not in _NC_CACHE:
        _NC_CACHE[key] = build_program(n_layers, passes)
    nc = _NC_CACHE[key]
    ropec, ropes = _rope_tables()
    kl = np.arange(128)[:, None]
    ql = np.arange(128)[None, :]
    bandm = np.concatenate([(kl <= ql), np.ones((128, 128), bool), (ql <= kl)], 1).astype(np.float32)
    shared = {
        "ada_w": f(ada_w), "ada_b": f(ada_b), "gmix": f(norm_mix_g), "gmlp": f(norm_mlp_g), "w_o": f(w_o),
        "w1": f(mlp_w1), "b1": f(mlp_b1), "w2": f(mlp_w2), "b2": f(mlp_b2),
        "wqa": f(w_qkv_a), "wqb": f(w_qkv_b), "wqc": f(w_qkv_c), "sink": f(sink_b), "qn": f(q_norm_c),
        "kn": f(k_norm_c), "gfin": f(final_norm_g), "rpbt": _rpb_tables(f(rpb_a)),
        "ident": np.eye(128, dtype=np.float32), "ropec": ropec, "ropes": ropes, "bandm": bandm,
    }
    xp, xs = f(x_prompt), f(x_sample)
    in_maps = []
    for i in range(NCORES):
        mp = dict(shared)
        mp["xp"] = xp[4 * i:4 * i + 4].reshape(1024, D)
        mp["xs"] = xs[i]
        mp["cka"] = f(cache_k_a)[i].reshape(2, 512, 1024); mp["cva"] = f(cache_v_a)[i].reshape(2, 512, 1024)
        mp["ckb"] = f(cache_k_b)[i].reshape(1, 512, 256); mp["cvb"] = f(cache_v_b)[i].reshape(1, 512, 256)
        mp["ckc"] = f(cache_k_c)[i].reshape(1, 512, 256); mp["cvc"] = f(cache_v_c)[i].reshape(1, 512, 256)
        mp["cond"] = np.ascontiguousarray(np.stack([f(c_ctx), f(c)[i]], 0))
        in_maps.append(mp)
    res = run_bass_kernel_spmd(nc, in_maps, core_ids=list(range(NCORES)))
    R = res.results
    cat = lambda k: np.concatenate([np.asarray(R[i][k], np.float32) for i in range(NCORES)], 0)
    y_prompt = cat("yp").reshape(32, 256, D)
    y_sample = np.stack([np.asarray(R[i]["ys"], np.float32) for i in range(NCORES)], 0)
    k_a = cat("ka").reshape(32, 2, 256, 16, 64); v_a = cat("va").reshape(32, 2, 256, 16, 64)
    k_b = cat("kb").reshape(32, 1, 256, 4, 64); v_b = cat("vb").reshape(32, 1, 256, 4, 64)
    k_c = cat("kc").reshape(32, 1, 256, 4, 64); v_c = cat("vc").reshape(32, 1, 256, 4, 64)
    return (y_prompt, y_sample, k_a, v_a, k_b, v_b, k_c, v_c)
```

```python
from contextlib import ExitStack
import os
import numpy as np
import concourse.bass as bass
import concourse.mybir as mybir
from concourse.bass_utils import run_bass_kernel_spmd

F32 = mybir.dt.float32
BF16 = mybir.dt.bfloat16
ALU = mybir.AluOpType
AF = mybir.ActivationFunctionType

D = 1024
NCH = 8
DFF = 4096
NEG = -30000.0
EPS = 1e-6
NCORES = 8


class Dep:
    __slots__ = ("w", "r", "excl")

    def __init__(self, excl=False):
        self.w = None
        self.r = {}
        self.excl = excl


class Prog:
    ENGS = ("pe", "act", "dve", "pool", "sp")
    DMA_RING = 8

    def __init__(self, nc, es):
        self.nc = nc
        self.ops = {e: [] for e in self.ENGS}
        self.cnt = {e: 0 for e in self.ENGS}
        self.known = {e: {} for e in self.ENGS}
        self.sems = {}
        for e in self.ENGS:
            self.sems[e] = es.enter_context(nc.semaphore("s_" + e))
        self.dring, self.dval, self.dpos = {}, {}, {}
        for q in ("sp", "act", "pool"):
            self.dring[q] = []
            for i in range(self.DMA_RING):
                k = "d_%s_%d" % (q, i)
                self.sems[k] = es.enter_context(nc.semaphore(k))
                self.dval[k] = 0
                self.dring[q].append(k)
            self.dpos[q] = 0
        self.n_inst = 0

    def _need(self, eng, ev, needs):
        if ev is None:
            return
        k, v = ev
        if eng == "pe" and k == "pe":
            return
        if self.known[eng].get(k, 0) >= v:
            return
        if needs.get(k, 0) < v:
            needs[k] = v

    def _collect(self, eng, reads, writes, is_dma=False):
        needs = {}
        for d in reads:
            self._need(eng, d.w, needs)
            if d.excl:
                for k, v in d.r.items():
                    if k != eng:
                        self._need(eng, (k, v), needs)
        for d in writes:
            self._need(eng, d.w, needs)
            for k, v in d.r.items():
                self._need(eng, (k, v), needs)
        return needs

    def _emit_waits(self, eng, needs):
        for k, v in needs.items():
            if k in self.cnt:
                assert self.cnt[k] >= v, "wait on unsignaled event %s %d>%d" % (k, v, self.cnt[k])
            sem = self.sems[k]
            self.ops[eng].append(lambda e, sem=sem, v=v: e.wait_ge(sem, v))
            self.known[eng][k] = v
            self.n_inst += 1

    def op(self, eng, name, reads=(), writes=(), signal=True, **kw):
        fn = lambda e, name=name, kw=kw: getattr(e, name)(**kw)
        needs = self._collect(eng, reads, writes)
        self._emit_waits(eng, needs)
        val = self.cnt[eng] + 1
        for d in reads:
            d.r[eng] = val
        for d in writes:
            d.w = (eng, val)
            d.r = {}
        if signal:
            self.cnt[eng] = val
            sem = self.sems[eng]
            self.ops[eng].append(lambda e, fn=fn, sem=sem: fn(e).then_inc(sem, 1))
        else:
            self.ops[eng].append(lambda e, fn=fn: fn(e))
        self.n_inst += 1

    def dma(self, q, out, in_, reads=(), writes=(), **kw):
        needs = self._collect(q, reads, writes, is_dma=True)
        k = self.dring[q][self.dpos[q]]
        self.dpos[q] = (self.dpos[q] + 1) % len(self.dring[q])
        if self.dval[k] > 0:
            self._need(q, (k, self.dval[k]), needs)
        self._emit_waits(q, needs)
        self.dval[k] += 16
        v = self.dval[k]
        for d in reads:
            d.r[k] = v
        for d in writes:
            d.w = (k, v)
            d.r = {}
        sem = self.sems[k]
        self.ops[q].append(
            lambda e, out=out, in_=in_, sem=sem, kw=kw: e.dma_start(out=out, in_=in_, **kw).then_inc(sem, 16))
        self.n_inst += 1

    def barrier(self):
        for e in self.ENGS:
            needs = {}
            for k in self.ENGS:
                if k != e and self.cnt[k] > 0:
                    self._need(e, (k, self.cnt[k]), needs)
            for k, v in self.dval.items():
                if v > 0:
                    self._need(e, (k, v), needs)
            self._emit_waits(e, needs)

    def run(self):
        self.barrier()
        with self.nc.Block() as block:
            ops = self.ops

            @block.tensor
            def _(e):
                for f in ops["pe"]:
                    f(e)

            @block.scalar
            def _(e):
                for f in ops["act"]:
                    f(e)

            @block.vector
            def _(e):
                for f in ops["dve"]:
                    f(e)

            @block.gpsimd
            def _(e):
                for f in ops["pool"]:
                    f(e)

            @block.sync
            def _(e):
                for f in ops["sp"]:
                    f(e)


class Ring:
    def __init__(self, items):
        self.items = items
        self.i = 0

    def next(self):
        it = self.items[self.i]
        self.i = (self.i + 1) % len(self.items)
        return it


class Prefetch:
    def __init__(self, ring, issue, items, depth):
        self.ring, self.issue, self.items, self.depth = ring, issue, items, depth
        self.nxt = 0
        self.slots = {}

    def get(self, i):
        while self.nxt < min(len(self.items), i + self.depth + 1):
            tile, dep = self.ring.next()
            self.issue(tile, dep, self.items[self.nxt])
            self.slots[self.nxt] = (tile, dep)
            self.nxt += 1
        return self.slots.pop(i)

    def prime(self):
        self.slots[-1] = None
        self.get(-1)


def build_program(n_layers=4, passes=("P", "S")):
    nc = bass.Bass("TRN2", target_bir_lowering=False)

    def din(name, shape, dt=F32):
        return nc.dram_tensor(name, list(shape), dt, kind="ExternalInput").ap()

    def dout(name, shape):
        return nc.dram_tensor(name, list(shape), F32, kind="ExternalOutput").ap()

    I = {}
    I["xp"] = din("xp", [1024, D])
    I["xs"] = din("xs", [2048, D])
    I["cka"] = din("cka", [2, 512, 1024]); I["cva"] = din("cva", [2, 512, 1024])
    I["ckb"] = din("ckb", [1, 512, 256]); I["cvb"] = din("cvb", [1, 512, 256])
    I["ckc"] = din("ckc", [1, 512, 256]); I["cvc"] = din("cvc", [1, 512, 256])
    I["cond"] = din("cond", [2, D])
    I["ada_w"] = din("ada_w", [4, D, 6 * D]); I["ada_b"] = din("ada_b", [4, 6 * D])
    I["gmix"] = din("gmix", [4, D]); I["gmlp"] = din("gmlp", [4, D])
    I["w_o"] = din("w_o", [4, D, D])
    I["w1"] = din("w1", [4, D, DFF]); I["b1"] = din("b1", [4, DFF])
    I["w2"] = din("w2", [4, DFF, D]); I["b2"] = din("b2", [4, D])
    I["wqa"] = din("wqa", [2, D, 3072]); I["wqb"] = din("wqb", [1, D, 1536]); I["wqc"] = din("wqc", [1, D, 1536])
    I["sink"] = din("sink", [1, 16])
    I["qn"] = din("qn", [1, 64]); I["kn"] = din("kn", [1, 64])
    I["gfin"] = din("gfin", [D])
    I["rpbt"] = din("rpbt", [2, 16, 2, 128, 896])
    I["ident"] = din("ident", [128, 128])
    I["ropec"] = din("ropec", [128, 2048]); I["ropes"] = din("ropes", [128, 2048])
    I["bandm"] = din("bandm", [128, 384])
    O = {}
    O["yp"] = dout("yp", [1024, D]); O["ys"] = dout("ys", [2048, D])
    O["ka"] = dout("ka", [4, 2, 256, 1024]); O["va"] = dout("va", [4, 2, 256, 1024])
    O["kb"] = dout("kb", [4, 1, 256, 256]); O["vb"] = dout("vb", [4, 1, 256, 256])
    O["kc"] = dout("kc", [4, 1, 256, 256]); O["vc"] = dout("vc", [4, 1, 256, 256])

    with ExitStack() as es:
        P = Prog(nc, es)

        def sb(name, shape, dt):
            return es.enter_context(nc.sbuf_tensor("sb_" + name, list(shape), dt))

        xT = sb("xT", [128, NCH, 2048], F32)
        hT = sb("hT", [128, NCH, 2048], BF16)
        ropeC = sb("ropeC", [128, 2048], F32)
        ropeS = sb("ropeS", [128, 2048], F32)
        ident = sb("ident", [128, 128], F32)
        ones_bf = sb("ones_bf", [128, 512], BF16)
        bones_bf = sb("bones_bf", [128, 128], BF16)
        bandm = sb("bandm", [128, 384], BF16)
        condT = sb("condT", [128, NCH, 2], F32)
        sT = sb("sT", [128, NCH, 2], BF16)
        mods = sb("mods", [128, 4, 48, 2], F32)
        adabT = sb("adabT", [128, 4, 48], F32)
        gmixT = sb("gmixT", [128, 4, NCH], F32)
        gmlpT = sb("gmlpT", [128, 4, NCH], F32)
        b1T = sb("b1T", [128, 4, 32], F32)
        b2T = sb("b2T", [128, 4, NCH], F32)
        gfinT = sb("gfinT", [128, NCH], F32)
        gA = sb("gA", [128, 4, NCH, 2], F32)
        gFm = sb("gFm", [128, 4, NCH, 2], F32)
        b2g = sb("b2g", [128, 4, NCH, 2], F32)
        qn128 = sb("qn128", [128, 4], F32)
        knb = sb("knb", [128, 64], F32)
        es16 = sb("es16", [1, 16], F32)
        rstd = sb("rstd", [128, 512], F32)
        rsum_t = [sb("rsum%d" % i, [128, 512], F32) for i in range(2)]
        tmpf = [sb("tmpf%d" % i, [128, 512], F32) for i in range(3)]
        ptb = [sb("ptb%d" % i, [128, 512], BF16) for i in range(5)]
        ARENA_B = 56896
        arena = sb("arena", [128, ARENA_B // 2], BF16)

        d_const = Dep()
        d_mods = Dep()
        d_rstd = Dep()
        rsum_ring = Ring([(rsum_t[i], Dep()) for i in range(2)])
        tmpf_ring = Ring([(tmpf[i], Dep()) for i in range(3)])
        pt_ring = Ring([(ptb[i], Dep()) for i in range(5)])

        class Arena:
            def __init__(self):
                self.off = 0

            def reset(self):
                self.off = 0

            def alloc(self, shape, dt):
                n = int(np.prod(shape))
                nb = n * (4 if dt == F32 else 2)
                off = (self.off + 63) // 64 * 64
                assert off + nb <= ARENA_B, "arena overflow %d" % (off + nb)
                self.off = off + nb
                v = arena[:, off // 2: (off + nb) // 2]
                if dt == F32:
                    v = v.bitcast(F32)
                if len(shape) == 2:
                    v = v.rearrange("p (a b) -> p a b", a=shape[0])
                elif len(shape) == 3:
                    v = v.rearrange("p (a b c) -> p a b c", a=shape[0], b=shape[1])
                return v

        AR = Arena()

        psb = [es.enter_context(nc.psum_tensor("ps%d" % i, [128, 512], F32)) for i in range(8)]
        psd = [Dep(excl=True) for _ in range(8)]
        ps_s = Ring([(psb[i], psd[i]) for i in range(0, 4)])
        ps_o = Ring([(psb[i], psd[i]) for i in range(4, 6)])
        ps_m = Ring([(psb[i], psd[i]) for i in range(6, 8)])

        def mm(out, lhsT, rhs, start, stop, reads, writes, signal=None):
            if signal is None:
                signal = stop
            P.op("pe", "matmul", reads=reads, writes=writes, signal=signal, out=out, lhsT=lhsT, rhs=rhs,
                 start=start, stop=stop, skip_group_check=True)

        def tr(out, in_, reads, writes, signal=True):
            P.op("pe", "transpose", reads=reads, writes=writes, signal=signal, out=out, in_=in_, identity=ident[:])

        def act(out, in_, func, reads, writes, bias=None, scale=None):
            kw = dict(out=out, in_=in_, func=func)
            if bias is not None:
                kw["bias"] = bias
            if scale is not None:
                kw["scale"] = scale
            P.op("act", "activation", reads=reads, writes=writes, **kw)

        def tt(eng, out, in0, in1, op, reads, writes):
            P.op(eng, "tensor_tensor", reads=reads, writes=writes, out=out, in0=in0, in1=in1, op=op)

        def stt(eng, out, in0, scalar, in1, op0, op1, reads, writes):
            P.op(eng, "scalar_tensor_tensor", reads=reads, writes=writes, out=out, in0=in0, scalar=scalar, in1=in1,
                 op0=op0, op1=op1)

        nck = dict(allow_slow_non_contiguous=True)
        P.dma("sp", ident[:], I["ident"])
        P.op("pool", "memset", ap=ones_bf[:], constant=1.0)
        d_bones, d_sink = Dep(), Dep()
        P.op("pool", "memset", writes=[d_bones], ap=bones_bf[:], constant=0.0)
        P.op("pool", "memset", writes=[d_bones], ap=bones_bf[0:64, 0:64], constant=1.0)
        P.op("pool", "memset", writes=[d_bones], ap=bones_bf[64:128, 64:128], constant=1.0)
        P.dma("pool", bandm[:], I["bandm"])
        P.dma("sp", ropeC[:], I["ropec"])
        P.dma("sp", ropeS[:], I["ropes"])
        for ci in range(2):
            P.dma("sp", condT[:, :, ci], I["cond"][ci].rearrange("(k p) -> p k", p=128), **nck)
        for l in range(4):
            P.dma("sp", adabT[:, l, :], I["ada_b"][l].rearrange("(c p) -> p c", p=128), **nck)
            P.dma("sp", gmixT[:, l, :], I["gmix"][l].rearrange("(k p) -> p k", p=128), **nck)
            P.dma("sp", gmlpT[:, l, :], I["gmlp"][l].rearrange("(k p) -> p k", p=128), **nck)
            P.dma("sp", b1T[:, l, :], I["b1"][l].rearrange("(c p) -> p c", p=128), **nck)
            P.dma("sp", b2T[:, l, :], I["b2"][l].rearrange("(k p) -> p k", p=128), **nck)
        P.dma("sp", gfinT[:], I["gfin"].rearrange("(k p) -> p k", p=128), **nck)
        for half in range(2):
            for ti, nm in ((0, "qn"), (2, "kn")):
                src = I[nm].rearrange("o d -> d o")
                P.dma("sp", qn128[half * 64:(half + 1) * 64, ti:ti + 1], src, **nck)
                for blk in range(4):
                    pb = blk ^ 1
                    P.dma("sp", qn128[half * 64 + blk * 16: half * 64 + blk * 16 + 16, ti + 1:ti + 2],
                          src[pb * 16:(pb + 1) * 16, :], **nck)
        P.dma("sp", knb[:], I["kn"].to_broadcast([128, 64]), **nck)
        P.dma("sp", es16[:], I["sink"])
        P.barrier()
        act(es16[:], es16[:], AF.Exp, reads=[d_const], writes=[d_const])

        STOP = int(os.environ.get("MK_STOP", "99"))
        DBG = int(os.environ.get("MK_DBG", "0"))
        sg, sgd = tmpf_ring.next()
        condF = condT[:].rearrange("p k c -> p (k c)")
        act(sg[:, 0:16], condF, AF.Exp, reads=[], writes=[sgd], scale=-1.0)
        P.op("dve", "tensor_scalar_add", reads=[sgd], writes=[sgd], out=sg[:, 0:16], in0=sg[:, 0:16], scalar1=1.0)
        P.op("dve", "reciprocal", reads=[sgd], writes=[sgd], out=sg[:, 0:16], in_=sg[:, 0:16])
        tt("dve", sT[:].rearrange("p k c -> p (k c)"), sg[:, 0:16], condF, ALU.mult, reads=[sgd], writes=[d_mods])
        d_sT = d_mods
        d_modl = [Dep() for _ in range(4)]

        def ada_alloc(l):
            ring = Ring([(AR.alloc([NCH, 768], BF16), Dep()) for _ in range(2)])
            pf = Prefetch(ring, lambda wt, wd, p_: P.dma(
                "pool", wt, I["ada_w"][l][:, p_ * 768:(p_ + 1) * 768].rearrange("(k p) n -> p k n", p=128),
                writes=[wd]), list(range(8)), 1)
            pf.prime()
            return pf

        def ada_piece(l, pf, piece):
            wt, wd = pf.get(piece)
            pt, pdp = ps_m.next()
            for cc in range(6):
                for k in range(NCH):
                    mm(pt[:, cc * 2: cc * 2 + 2], wt[:, k, cc * 128:(cc + 1) * 128], sT[:, k, :],
                       start=(k == 0), stop=(k == NCH - 1), reads=[wd, d_sT], writes=[pdp])
            tt("dve", mods[:, l, piece * 6:(piece + 1) * 6, :], pt[:, 0:12].rearrange("p (c t) -> p c t", t=2),
               adabT[:, l, piece * 6:(piece + 1) * 6].unsqueeze(2).to_broadcast([128, 6, 2]), ALU.add, reads=[pdp],
               writes=[d_modl[l]])
            if piece == 7:
                stt("dve", gA[:, l, :, :], mods[:, l, 8:16, :], 1.0,
                    gmixT[:, l, :].unsqueeze(2).to_broadcast([128, NCH, 2]), ALU.add, ALU.mult, reads=[d_modl[l]],
                    writes=[d_modl[l]])
                stt("dve", gFm[:, l, :, :], mods[:, l, 32:40, :], 1.0,
                    gmlpT[:, l, :].unsqueeze(2).to_broadcast([128, NCH, 2]), ALU.add, ALU.mult, reads=[d_modl[l]],
                    writes=[d_modl[l]])
                tt("dve", b2g[:, l, :, :], mods[:, l, 40:48, :], b2T[:, l, :].unsqueeze(2).to_broadcast([128, NCH, 2]),
                   ALU.mult, reads=[d_modl[l]], writes=[d_modl[l]])

        lazy_ada = passes[0] == "P"
        AR.reset()
        for l in range(1 if lazy_ada else n_layers):
            pf0 = ada_alloc(l)
            for piece in range(8):
                ada_piece(l, pf0, piece)
            P.barrier()
            AR.reset()
        P.barrier()

        def ckpt(n):
            if STOP <= n:
                raise StopIteration

        def rms_stats(srcs, tcols, ones_t, scale, out_t, out_d, src_psum=False):
            pt, pdp = ps_m.next()
            nk = len(srcs)
            for k, (src, sdeps) in enumerate(srcs):
                sq, sqd = pt_ring.next()
                if k % 2 == 0 or src_psum:
                    act(sq[:, 0:tcols], src, AF.Square, reads=sdeps, writes=[sqd])
                else:
                    tt("dve", sq[:, 0:tcols], src, src, ALU.mult, reads=sdeps, writes=[sqd])
                mm(pt[:, 0:tcols], ones_t, sq[:, 0:tcols], start=(k == 0), stop=(k == nk - 1),
                   reads=[sqd], writes=[pdp], signal=True)
            act(out_t[:, 0:tcols], pt[:, 0:tcols], AF.Ln, reads=[pdp], writes=[out_d], bias=EPS, scale=scale)
            act(out_t[:, 0:tcols], out_t[:, 0:tcols], AF.Exp, reads=[out_d], writes=[out_d], scale=-0.5)

        def run_pass(grp):
            ckpt(1)
            sample = grp == "S"
            NT = 2048 if sample else 1024
            NTC = NT // 512
            NTT = NT // 128
            ci = 1 if sample else 0
            xin = I["xs"] if sample else I["xp"]
            yout = O["ys"] if sample else O["yp"]
            xd = [[Dep() for _ in range(NTC)] for _ in range(NCH)]
            hd = [[Dep() for _ in range(NCH)] for _ in range(NTC)]

            AR.reset()
            stage = [AR.alloc([1024], F32) for _ in range(2)]
            st_ring = Ring([(stage[i], Dep()) for i in range(2)])
            for t_ in range(NTT):
                stg, sd = st_ring.next()
                P.dma("sp", stg, xin[t_ * 128:(t_ + 1) * 128, :], writes=[sd])
                for half in range(2):
                    pt, pdp = ps_s.next()
                    for kk in range(4):
                        k = half * 4 + kk
                        tr(pt[:, kk * 128:(kk + 1) * 128], stg[:, k * 128:(k + 1) * 128], reads=[sd], writes=[pdp],
                           signal=(kk == 3))
                    tc = t_ // 4
                    dst = xT[:, half * 4:(half + 1) * 4, t_ * 128:(t_ + 1) * 128]
                    src = pt[:, :].rearrange("p (k t) -> p k t", k=4)
                    wr = [xd[half * 4 + kk][tc] for kk in range(4)]
                    if half == 0:
                        P.op("act", "copy", reads=[pdp], writes=wr, out=dst, in_=src)
                    else:
                        P.op("dve", "tensor_copy", reads=[pdp], writes=wr, out=dst, in_=src)
            P.barrier()
            ckpt(2)

            def norm_to_h(tc, gs, sh):
                cols = slice(tc * 512, (tc + 1) * 512)
                rms_stats([(xT[:, k, cols], [xd[k][tc]]) for k in range(NCH)], 512, ones_bf[:, 0:128], 1.0 / D,
                          rstd, d_rstd)
                for k in range(NCH):
                    t, td = tmpf_ring.next()
                    tt("dve", t[:], xT[:, k, cols], rstd[:], ALU.mult, reads=[xd[k][tc], d_rstd], writes=[td])
                    act(hT[:, k, cols], t[:], AF.Identity, reads=[td], writes=[hd[tc][k]], bias=sh[:, k:k + 1],
                        scale=gs[:, k:k + 1])

            for l in range(n_layers):
                m, j = l % 3, l // 3
                wqkv = (I["wqa"], I["wqb"], I["wqc"])[m][j]
                gqa = m != 0
                rope = sample and m != 0
                qknorm = m == 2
                k_out = (O["ka"], O["kb"], O["kc"])[m]
                v_out = (O["va"], O["vb"], O["vc"])[m]
                ck = (I["cka"], I["ckb"], I["ckc"])[m][j]
                cv = (I["cva"], I["cvb"], I["cvc"])[m][j]

                ckpt(3)
                AR.reset()
                NK = NT + (512 if sample else 0)
                NKT = NK // 128
                QTz = AR.alloc([2, NT], BF16)
                d_qz = Dep()
                P.op("pool", "memset", writes=[d_qz], ap=QTz[:, :, :], constant=0.0)
                sinkrow = None
                if m == 1:
                    sinkrow = AR.alloc([16, 128], BF16)
                    d_sk = Dep()
                    P.op("dve", "memset", writes=[d_sk], ap=sinkrow[0:1, :, :], constant=0.0)
                    for h_ in range(16):
                        lo = 64 if h_ % 2 == 0 else 0
                        P.op("dve", "tensor_copy", reads=[], writes=[d_sk], out=sinkrow[0:1, h_, lo:lo + 64],
                             in_=es16[0:1, h_:h_ + 1].to_broadcast([1, 64]))
                KT = AR.alloc([NK], BF16)
                VA = AR.alloc([NKT, 192], BF16)
                oT = AR.alloc([NT], BF16)
                wq = AR.alloc([NCH, 128], BF16); wk = AR.alloc([NCH, 128], BF16); wv = AR.alloc([NCH, 128], BF16)
                wo = AR.alloc([1024], BF16)
                wqr = wkr = None
                if rope:
                    wqr = AR.alloc([NCH, 128], BF16); wkr = AR.alloc([NCH, 128], BF16)
                kst = AR.alloc([4, 128], F32)
                d_kst = Dep()
                ko_ring = vo_ring = bias_ring = None
                if not sample:
                    ko_ring = Ring([(AR.alloc([128], F32), Dep()) for _ in range(3)])
                    vo_ring = Ring([(AR.alloc([128], F32), Dep()) for _ in range(3)])
                if sample and m == 0:
                    bias_ring = Ring([(AR.alloc([2, 896], F32), Dep()) for _ in range(3)])
                d_wq, d_wk, d_wv, d_wo, d_wqr, d_wkr = [Dep() for _ in range(6)]
                ada_next = None
                if lazy_ada and grp == passes[0] and l + 1 < n_layers:
                    ada_next = ada_alloc(l + 1)
                qd = [[Dep(), Dep()] for _ in range(NTC)]
                kd = [Dep() for _ in range(NTC + 1)]
                vd = [[Dep(), Dep()] for _ in range(NKT)]
                odmap = {}
                pending_wo = {tc_: [] for tc_ in range(NTC)}
                wo_state = {"next": None}

                def pop_wo(tc_only=None):
                    if tc_only is not None:
                        items = pending_wo[tc_only]
                        pending_wo[tc_only] = []
                        for it in items:
                            it()
                    else:
                        for tc_ in range(NTC):
                            if pending_wo[tc_]:
                                pending_wo[tc_].pop(0)()
                                break
                    if wo_state["next"] is not None and not any(pending_wo.values()):
                        load_wo(wo_state["next"])
                        wo_state["next"] = None
                d_ones = Dep()
                P.op("pool", "memset", writes=[d_ones], ap=VA[:, :, 64:128], constant=1.0)
                wsrc = wqkv.rearrange("(k p) n -> p k n", p=128)

                def load_qkv(c):
                    g = c // 2
                    P.dma("pool", wq, wsrc[:, :, c * 128:(c + 1) * 128], writes=[d_wq])
                    if gqa and c % 2 == 1:
                        return
                    if gqa:
                        for hf in range(2):
                            P.dma("pool", wk[:, :, hf * 64:(hf + 1) * 64], wsrc[:, :, 1024 + g * 64:1024 + (g + 1) * 64],
                                  writes=[d_wk])
                            P.dma("pool", wv[:, :, hf * 64:(hf + 1) * 64], wsrc[:, :, 1280 + g * 64:1280 + (g + 1) * 64],
                                  writes=[d_wv])
                    else:
                        P.dma("pool", wk, wsrc[:, :, 1024 + c * 128:1024 + (c + 1) * 128], writes=[d_wk])
                        P.dma("pool", wv, wsrc[:, :, 2048 + c * 128:2048 + (c + 1) * 128], writes=[d_wv])

                def load_wo(c):
                    P.dma("pool", wo, I["w_o"][l][c * 128:(c + 1) * 128, :], writes=[d_wo])

                def load_kst(c):
                    g = c // 2
                    if gqa:
                        for hf in range(2):
                            P.dma("sp", kst[:, :, hf * 64:(hf + 1) * 64],
                                  ck[:, g * 64:(g + 1) * 64].rearrange("(t p) d -> p t d", p=128), writes=[d_kst])
                    else:
                        P.dma("sp", kst[:, :, :], ck[:, c * 128:(c + 1) * 128].rearrange("(t p) d -> p t d", p=128),
                              writes=[d_kst])

                bias_pf = None
                if sample and m == 0:
                    bias_pf = Prefetch(bias_ring, lambda bt, btd, h: P.dma(
                        "sp", bt, I["rpbt"][j, h].rearrange("k p n -> p k n"), writes=[btd]), list(range(16)), 1)
                    bias_pf.prime()
                load_qkv(0)
                load_wo(0)
                if sample:
                    load_kst(0)

                for tc in range(NTC):
                    norm_to_h(tc, gA[:, l, :, ci], mods[:, l, 0:8, ci])

                for c in range(8):
                    g = c // 2
                    if rope:
                        for (src_t, dst_t, sdp, ddp) in (((wq, wqr, d_wq, d_wqr), (wk, wkr, d_wk, d_wkr))
                                                         if ((not gqa) or c % 2 == 0) else ((wq, wqr, d_wq, d_wqr),)):
                            sv = src_t.rearrange("p k (a b r) -> p k a b r", a=4, b=2)
                            dv = dst_t.rearrange("p k (a b r) -> p k a b r", a=4, b=2)
                            for b_ in range(2):
                                P.op("dve", "tensor_copy", reads=[sdp], writes=[ddp], out=dv[:, :, :, b_, :],
                                     in_=sv[:, :, :, 1 - b_, :])

                    newkv = (not gqa) or (c % 2 == 0)

                    def proj_T(dst_fn, dstd, w_t, wd_, wr_t, wrd_, gcol, split):
                        for tc in range(NTC):
                            cols = slice(tc * 512, (tc + 1) * 512)
                            pq, pqd = ps_s.next()
                            for k in range(NCH):
                                mm(pq[:], w_t[:, k, :], hT[:, k, cols], start=(k == 0), stop=(k == NCH - 1),
                                   reads=[wd_, hd[tc][k]], writes=[pqd])
                            if rope:
                                pr, prd = ps_s.next()
                                for k in range(NCH):
                                    mm(pr[:], wr_t[:, k, :], hT[:, k, cols], start=(k == 0), stop=(k == NCH - 1),
                                       reads=[wrd_, hd[tc][k]], writes=[prd])
                            if qknorm:
                                rms_stats([(pq[:], [pqd])], 512, bones_bf[:], 1.0 / 64, rstd, d_rstd, src_psum=True)
                            halves = [(0, slice(0, 64)), (1, slice(64, 128))] if split else [(None, slice(0, 128))]

                            def wdep(hh_):
                                return [dstd[tc][hh_]] if split else [dstd[tc]]
                            if rope:
                                t1, t1d = tmpf_ring.next()
                                t2, t2d = tmpf_ring.next()
                                if qknorm:
                                    stt("dve", t1[:], pq[:], qn128[:, gcol:gcol + 1], ropeC[:, cols], ALU.mult, ALU.mult,
                                        reads=[pqd], writes=[t1d])
                                    stt("dve", t2[:], pr[:], qn128[:, gcol + 1:gcol + 2], ropeS[:, cols], ALU.mult,
                                        ALU.mult, reads=[prd], writes=[t2d])
                                    tt("pool", t1[:], t1[:], t2[:], ALU.add, reads=[t1d, t2d], writes=[t1d])
                                    for hh_, prt in halves:
                                        tt("pool", dst_fn(hh_, prt, cols), t1[prt, :], rstd[prt, :], ALU.mult,
                                           reads=[t1d, d_rstd], writes=wdep(hh_))
                                else:
                                    tt("dve", t1[:], pq[:], ropeC[:, cols], ALU.mult, reads=[pqd], writes=[t1d])
                                    tt("dve", t2[:], pr[:], ropeS[:, cols], ALU.mult, reads=[prd], writes=[t2d])
                                    for hh_, prt in halves:
                                        tt("pool", dst_fn(hh_, prt, cols), t1[prt, :], t2[prt, :], ALU.add,
                                           reads=[t1d, t2d], writes=wdep(hh_))
                            elif qknorm:
                                for hh_, prt in halves:
                                    stt("dve", dst_fn(hh_, prt, cols), pq[prt, :], qn128[prt, gcol:gcol + 1], rstd[prt, :],
                                        ALU.mult, ALU.mult, reads=[pqd, d_rstd], writes=wdep(hh_))
                            else:
                                for hh_, prt in halves:
                                    P.op("act", "copy", reads=[pqd], writes=wdep(hh_), out=dst_fn(hh_, prt, cols),
                                         in_=pq[prt, :])

                    proj_T(lambda hh_, prt, cols: QTz[prt, hh_, cols], qd, wq, d_wq, wqr, d_wqr, 0, True)
                    if newkv:
                        proj_T(lambda hh_, prt, cols: KT[prt, cols], kd, wk, d_wk, wkr, d_wkr, 2, False)
                    ckpt(4)

                    for t_ in (range(NTT) if newkv else ()):
                        tc = t_ // 4
                        tcols = slice(t_ * 128, (t_ + 1) * 128)
                        pv, pvd = ps_s.next()
                        for k in range(NCH):
                            mm(pv[:, 0:128], hT[:, k, tcols], wv[:, k, :], start=(k == 0), stop=(k == NCH - 1),
                               reads=[d_wv, hd[tc][k]], writes=[pvd])
                        do_out = (not sample) and ((not gqa) or c % 2 == 0)
                        if do_out:
                            for k in range(NCH):
                                mm(pv[:, 128:256], hT[:, k, tcols], wk[:, k, :], start=(k == 0), stop=(k == NCH - 1),
                                   reads=[d_wk, hd[tc][k]], writes=[pvd])
                        P.op("act", "copy", reads=[pvd], writes=[vd[t_][0]], out=VA[:, t_, 0:64], in_=pv[:, 0:64])
                        P.op("act", "copy", reads=[pvd], writes=[vd[t_][1]], out=VA[:, t_, 128:192], in_=pv[:, 64:128])
                        if do_out:
                            s_, r0 = t_ // 2, (t_ % 2) * 128
                            ncol = 64 if gqa else 128
                            c0 = g * 64 if gqa else c * 128
                            vo, vod = vo_ring.next()
                            P.op("dve", "tensor_copy", reads=[pvd], writes=[vod], out=vo[:, 0:ncol], in_=pv[:, 0:ncol])
                            if not (DBG & 1):
                                P.dma("sp", v_out[s_, j, r0:r0 + 128, c0:c0 + ncol], vo[:, 0:ncol], reads=[vod])
                            ko, kod = ko_ring.next()
                            if qknorm:
                                t, td = tmpf_ring.next()
                                P.op("dve", "tensor_copy", reads=[pvd], writes=[kod], out=ko[:, 0:64], in_=pv[:, 128:192])
                                tt("dve", t[:, 0:64], ko[:, 0:64], ko[:, 0:64], ALU.mult, reads=[kod], writes=[td])
                                P.op("dve", "tensor_reduce", reads=[td], writes=[td], out=t[:, 64:65], in_=t[:, 0:64],
                                     axis=mybir.AxisListType.X, op=ALU.add)
                                act(t[:, 64:65], t[:, 64:65], AF.Ln, reads=[td], writes=[td], bias=EPS, scale=1.0 / 64)
                                act(t[:, 64:65], t[:, 64:65], AF.Exp, reads=[td], writes=[td], scale=-0.5)
                                stt("dve", ko[:, 0:64], ko[:, 0:64], t[:, 64:65], knb[:], ALU.mult, ALU.mult,
                                    reads=[td, kod], writes=[kod])
                            else:
                                P.op("dve", "tensor_copy", reads=[pvd], writes=[kod], out=ko[:, 0:ncol],
                                     in_=pv[:, 128:128 + ncol])
                            if not (DBG & 1):
                                P.dma("sp", k_out[s_, j, r0:r0 + 128, c0:c0 + ncol], ko[:, 0:ncol], reads=[kod])

                    if c < 7:
                        load_qkv(c + 1)
                    if sample and newkv:
                        pk, pkd = ps_s.next()
                        for t4 in range(4):
                            tr(pk[:, t4 * 128:(t4 + 1) * 128], kst[:, t4, :], reads=[d_kst], writes=[pkd], signal=(t4 == 3))
                        P.op("act", "copy", reads=[pkd], writes=[kd[NTC]], out=KT[:, NT:NT + 512], in_=pk[:])
                        for hf in range(2):
                            csrc = cv[:, g * 64:(g + 1) * 64] if gqa else cv[:, c * 128 + hf * 64: c * 128 + hf * 64 + 64]
                            P.dma("pool", VA[:, NTT:NTT + 4, hf * 128: hf * 128 + 64],
                                  csrc.rearrange("(t p) d -> p t d", p=128), writes=[vd[NTT + t4][hf] for t4 in range(4)])

                    if sample and c < 7 and ((not gqa) or (c + 1) % 2 == 0):
                        load_kst(c + 1)
                    ckpt(5)

                    groups = []

                    def attend(hh, qlo, qn_, kparts, sinkh, tcq):
                        groups.append((hh, qlo, qn_, kparts, sinkh, tcq))

                    def run_groups():
                        LA = 3
                        flat = [(gi, pi) for gi, g_ in enumerate(groups) for pi in range(len(g_[3]))]
                        pend = {}
                        gstate = {}
                        for idx in range(len(flat) + LA):
                            if idx < len(flat):
                                gi, pi = flat[idx]
                                hh, qlo, qn_, kparts, sinkh, tcq = groups[gi]
                                hp = slice(hh * 64, (hh + 1) * 64)
                                kc, qo, ql, mode, extra = kparts[pi]
                                ps_, psd_ = ps_s.next()
                                ktc = min(kc // 4, NTC)
                                mm(ps_[:, 0:ql], KT[:, kc * 128:(kc + 1) * 128], QTz[:, hh, qlo + qo: qlo + qo + ql],
                                   start=True, stop=True, reads=[kd[ktc], qd[tcq][hh], d_qz], writes=[psd_])
                                pt_, ptd_ = pt_ring.next()
                                if mode == "bias":
                                    tb, tbd = tmpf_ring.next()
                                    bt, btd, j0 = extra
                                    stt("dve", tb[:, 0:ql], ps_[:, 0:ql], 0.125, bt[:, j0 * 64: j0 * 64 + ql], ALU.mult,
                                        ALU.add, reads=[psd_, btd], writes=[tbd])
                                    act(pt_[:, 0:ql], tb[:, 0:ql], AF.Exp, reads=[tbd], writes=[ptd_])
                                else:
                                    act(pt_[:, 0:ql], ps_[:, 0:ql], AF.Exp, reads=[psd_], writes=[ptd_], scale=0.125)
                                    if mode == "band":
                                        mo = extra
                                        for sub in range(ql // 128):
                                            mcol = mo + sub * 128
                                            if 128 <= mcol < 256:
                                                continue
                                            tt("dve", pt_[:, sub * 128:(sub + 1) * 128], pt_[:, sub * 128:(sub + 1) * 128],
                                               bandm[:, mcol:mcol + 128], ALU.mult, reads=[ptd_], writes=[ptd_])
                                pend[idx] = (pt_, ptd_)
                                pop_wo()
                            if idx >= LA:
                                gi, pi = flat[idx - LA]
                                hh, qlo, qn_, kparts, sinkh, tcq = groups[gi]
                                hp = slice(hh * 64, (hh + 1) * 64)
                                sp_ = slice((1 - hh) * 64, (2 - hh) * 64)
                                kc, qo, ql, mode, extra = kparts[pi]
                                pt_, ptd_ = pend.pop(idx - LA)
                                if pi == 0:
                                    gstate[gi] = ps_o.next()
                                po, pod = gstate[gi]
                                n = len(kparts)
                                last = (pi == n - 1) and sinkh is None
                                mm(po[:, qo:qo + ql], VA[:, kc, hh * 64: hh * 64 + 128], pt_[:, 0:ql],
                                   start=(pi == 0), stop=last, reads=vd[kc] + [ptd_, d_ones], writes=[pod],
                                   signal=(pi == n - 1) or (idx + 1 >= len(flat)))
                                if pi == n - 1:
                                    if sinkh is not None:
                                        mm(po[:, 0:qn_], sinkrow[0:1, sinkh, :], ones_bf[0:1, 0:qn_], start=False, stop=True,
                                           reads=[], writes=[pod])
                                    rsum, d_rsum = rsum_ring.next()
                                    act(rsum[sp_, 0:qn_], po[sp_, 0:qn_], AF.Ln, reads=[pod], writes=[d_rsum])
                                    act(rsum[sp_, 0:qn_], rsum[sp_, 0:qn_], AF.Exp, reads=[d_rsum], writes=[d_rsum],
                                        scale=-1.0)
                                    pop_wo(tc_only=tcq)
                                    odp = odmap.setdefault((tcq, hh, qlo), Dep())
                                    tt("dve", oT[hp, qlo:qlo + qn_], po[hp, 0:qn_], rsum[sp_, 0:qn_], ALU.mult,
                                       reads=[pod, d_rsum], writes=[odp])
                                    del gstate[gi]

                    for hh in range(2):
                        h = 2 * c + hh
                        sinkh = h if m == 1 else None
                        if not sample:
                            for s_ in range(4):
                                attend(hh, s_ * 256, 256,
                                       [(2 * s_, 0, 256, "plain", None), (2 * s_ + 1, 0, 256, "plain", None)], sinkh, s_ // 2)
                            continue
                        ctxp = [(NTT + t4, 0, 512, "plain", None) for t4 in range(4)]
                        if m == 0:
                            bt, btd = bias_pf.get(h)
                        for iq in range(4):
                            parts = list(ctxp)
                            if m == 2:
                                parts += [(kc, 0, 512, "plain", None) for kc in range(16)]
                            elif m == 1:
                                for mk in range(max(0, 4 * iq - 1), min(15, 4 * iq + 4) + 1):
                                    ulo, uhi = max(mk - 1, 4 * iq), min(mk + 1, 4 * iq + 3)
                                    parts.append((mk, (ulo - 4 * iq) * 128, (uhi - ulo + 1) * 128, "band",
                                                  (ulo - (mk - 1)) * 128))
                            else:
                                for mk in range(16):
                                    rlo = max(8 * iq, 4, 2 * mk - 3)
                                    rhi = min(8 * iq + 7, 28, 2 * mk + 5)
                                    if rlo <= rhi:
                                        parts.append((mk, (rlo - 8 * iq) * 64, (rhi - rlo + 1) * 64, "bias",
                                                      (bt[:, 0, :], btd, rlo - 2 * mk + 6)))
                                    if mk <= 3 and iq == 0:
                                        parts.append((mk, 0, 4 * 64, "bias", (bt[:, 1, :], btd, 0 - 2 * mk + 6)))
                                    if mk >= 12 and iq == 3:
                                        parts.append((mk, (29 - 24) * 64, 3 * 64, "bias",
                                                      (bt[:, 1, :], btd, 29 - 2 * mk + 6)))
                            attend(hh, iq * 512, 512, parts, sinkh, iq)

                    if ada_next is not None:
                        ada_piece(l + 1, ada_next, c)
                    run_groups()
                    ckpt(6)
                    def wo_item(tc, jo):
                        cols = slice(tc * 512, (tc + 1) * 512)
                        pw, pwd = ps_m.next()
                        mm(pw[:], wo[:, jo * 128:(jo + 1) * 128], oT[:, cols], start=True, stop=True,
                           reads=[d_wo] + [d_ for (t_k, _, _), d_ in odmap.items() if t_k == tc], writes=[pwd])
                        stt("dve", xT[:, jo, cols], pw[:], mods[:, l, 16 + jo, ci:ci + 1], xT[:, jo, cols], ALU.mult,
                            ALU.add, reads=[pwd, xd[jo][tc]], writes=[xd[jo][tc]])

                    for tc in range(NTC):
                        for jo in range(NCH):
                            pending_wo[tc].append(lambda tc=tc, jo=jo: wo_item(tc, jo))
                    wo_state["next"] = c + 1 if c < 7 else None
                    if c == 7:
                        for tc in range(NTC):
                            pop_wo(tc_only=tc)
                P.barrier()

                ckpt(7)
                AR.reset()
                h1T = AR.alloc([16, 1024], BF16)
                w1_ring = Ring([(AR.alloc([NCH, 128], BF16), Dep()) for _ in range(3)])
                w2_ring = Ring([(AR.alloc([16, 128], BF16), Dep()) for _ in range(2)])
                rl_ring = Ring([(AR.alloc([512], BF16), Dep()) for _ in range(3)])
                h1d = [[Dep() for _ in range(2)] for _ in range(16)]
                NB = NT // 1024
                w1_pf = Prefetch(w1_ring, lambda wt, wd, ff: P.dma(
                    "pool", wt, I["w1"][l][:, ff * 128:(ff + 1) * 128].rearrange("(k p) n -> p k n", p=128), writes=[wd]),
                    [hf_ * 16 + f_ for b_ in range(NB) for hf_ in range(2) for f_ in range(16)], 2)
                w2_pf = Prefetch(w2_ring, lambda wt, wd, it: P.dma(
                    "pool", wt, I["w2"][l][it[0] * 2048:(it[0] + 1) * 2048, it[1] * 128:(it[1] + 1) * 128]
                    .rearrange("(f p) n -> p f n", p=128), writes=[wd]),
                    [(hf_, jo_) for b_ in range(NB) for hf_ in range(2) for jo_ in range(NCH)], 1)
                w1_pf.prime()
                w2_pf.prime()
                for blk in range(NB):
                    for t2 in range(2):
                        norm_to_h(blk * 2 + t2, gFm[:, l, :, ci], mods[:, l, 24:32, ci])
                    for t2 in range(2):
                        tc = blk * 2 + t2
                        cols = slice(tc * 512, (tc + 1) * 512)
                        for jo in range(NCH):
                            P.op("dve", "tensor_scalar", reads=[xd[jo][tc]], writes=[xd[jo][tc]], out=xT[:, jo, cols],
                                 in0=xT[:, jo, cols], scalar1=b2g[:, l, jo, ci:ci + 1], scalar2=None, op0=ALU.add)
                    for half in range(2):
                        for f in range(16):
                            ff = half * 16 + f
                            wt, wd = w1_pf.get((blk * 2 + half) * 16 + f)
                            for t2 in range(2):
                                tc = blk * 2 + t2
                                cols = slice(tc * 512, (tc + 1) * 512)
                                ph, phd = ps_s.next()
                                for k in range(NCH):
                                    mm(ph[:], wt[:, k, :], hT[:, k, cols], start=(k == 0), stop=(k == NCH - 1),
                                       reads=[wd, hd[tc][k]], writes=[phd])
                                r_, rd_ = rl_ring.next()
                                act(r_[:], ph[:], AF.Relu, reads=[phd], writes=[rd_], bias=b1T[:, l, ff:ff + 1])
                                tt("dve", h1T[:, f, t2 * 512:(t2 + 1) * 512], r_[:], r_[:], ALU.mult, reads=[rd_],
                                   writes=[h1d[f][t2]])
                        for jo in range(NCH):
                            wt, wd = w2_pf.get((blk * 2 + half) * NCH + jo)
                            for t2 in range(2):
                                tc = blk * 2 + t2
                                cols = slice(tc * 512, (tc + 1) * 512)
                                py, pyd = ps_o.next()
                                for f in range(16):
                                    mm(py[:], wt[:, f, :], h1T[:, f, t2 * 512:(t2 + 1) * 512], start=(f == 0),
                                       stop=(f == 15), reads=[wd, h1d[f][t2]], writes=[pyd])
                                stt("dve", xT[:, jo, cols], py[:], mods[:, l, 40 + jo, ci:ci + 1], xT[:, jo, cols],
                                    ALU.mult, ALU.add, reads=[pyd, xd[jo][tc]], writes=[xd[jo][tc]])
                P.barrier()

            ckpt(8)
            AR.reset()
            os_ring = Ring([(AR.alloc([4, 1024], F32), [Dep() for _ in range(NCH)]) for _ in range(2)])
            for tc in range(NTC):
                cols = slice(tc * 512, (tc + 1) * 512)
                rms_stats([(xT[:, k, cols], [xd[k][tc]]) for k in range(NCH)], 512, ones_bf[:, 0:128], 1.0 / D,
                          rstd, d_rstd)
                osb, osd = os_ring.next()
                for k in range(NCH):
                    t, td = tmpf_ring.next()
                    stt("dve", t[:], xT[:, k, cols], gfinT[:, k:k + 1], rstd[:], ALU.mult, ALU.mult,
                        reads=[xd[k][tc], d_rstd], writes=[td])
                    pt, pdp = ps_s.next()
                    for t4 in range(4):
                        tr(pt[:, t4 * 128:(t4 + 1) * 128], t[:, t4 * 128:(t4 + 1) * 128], reads=[td], writes=[pdp],
                           signal=(t4 == 3))
                    P.op("act", "copy", reads=[pdp], writes=[osd[k]], out=osb[:, :, k * 128:(k + 1) * 128],
                         in_=pt[:, :].rearrange("p (a b) -> p a b", a=4))
                P.dma("sp", yout[tc * 512:(tc + 1) * 512, :].rearrange("(t p) n -> p t n", p=128), osb, reads=osd)
            P.barrier()

        try:
            for g_ in passes:
                run_pass(g_)
        except StopIteration:
            pass
        P.run()
        print("bass program: %d instructions" % P.n_inst, flush=True)
    return nc


def _rope_tables():
    t = np.arange(2048)
    rows = (t // 64).astype(np.float32)
    cols = (t % 64).astype(np.float32)
    freqs = np.exp(-np.log(np.float32(10000.0)) * np.arange(16, dtype=np.float32) / np.float32(16)).astype(np.float32)
    C = np.zeros((64, 2048), np.float32)
    S = np.zeros((64, 2048), np.float32)
    for d in range(64):
        pos = rows if d < 32 else cols
        i = d % 16
        ang = (pos * freqs[i]).astype(np.float32)
        C[d] = np.cos(ang)
        sgn = -1.0 if (d % 32) < 16 else 1.0
        S[d] = sgn * np.sin(ang)
    return np.concatenate([C, C], 0), np.concatenate([S, S], 0)


def _rpb_tables(rpb_a):
    krl = np.arange(2)[:, None, None, None]
    kc = np.arange(64)[None, :, None, None]
    jj = np.arange(14)[None, None, :, None]
    qc = np.arange(64)[None, None, None, :]
    a = krl - jj + 13 + 0 * kc + 0 * qc
    b = kc - qc + 15 + 0 * krl + 0 * jj
    c0 = np.clip(qc - 8, 0, 48)
    vcol = (kc >= c0) & (kc < c0 + 16) & (b >= 0) & (b <= 30)
    vcol = np.broadcast_to(vcol, a.shape)
    ac = np.clip(a, 0, 14)
    bc = np.clip(b, 0, 30)
    out = np.empty((2, 16, 2, 128, 14 * 64), np.float32)
    for kind in range(2):
        vrow = ((a >= 3) & (a <= 10)) if kind == 0 else ((a >= 0) & (a <= 14))
        valid = (vrow & vcol).reshape(128, 896)
        g = rpb_a[:, :, ac, bc].reshape(2, 16, 128, 896)
        out[:, :, kind] = np.where(valid[None, None], g, np.float32(NEG))
    return out


_NC_CACHE = {}


def kernel(x_prompt, x_sample, cache_k_a, cache_v_a, cache_k_b, cache_v_b, cache_k_c, cache_v_c,
           c, c_ctx, ada_w, ada_b, norm_mix_g, norm_mlp_g, w_o, mlp_w1, mlp_b1, mlp_w2, mlp_b2,
           w_qkv_a, rpb_a, w_qkv_b, sink_b, w_qkv_c, q_norm_c, k_norm_c, final_norm_g):
    f = lambda a: np.ascontiguousarray(np.asarray(a, dtype=np.float32))
    n_layers = int(os.environ.get("MK_LAYERS", "4"))
    passes = tuple(os.environ.get("MK_PASSES", "PS"))
    key = (n_layers, passes)
    if key not in _NC_CACHE:
        _NC_CACHE[key] = build_program(n_layers, passes)
    nc = _NC_CACHE[key]
    ropec, ropes = _rope_tables()
    kl = np.arange(128)[:, None]
    ql = np.arange(128)[None, :]
    bandm = np.concatenate([(kl <= ql), np.ones((128, 128), bool), (ql <= kl)], 1).astype(np.float32)
    shared = {
        "ada_w": f(ada_w), "ada_b": f(ada_b), "gmix": f(norm_mix_g), "gmlp": f(norm_mlp_g), "w_o": f(w_o),
        "w1": f(mlp_w1), "b1": f(mlp_b1), "w2": f(mlp_w2), "b2": f(mlp_b2),
        "wqa": f(w_qkv_a), "wqb": f(w_qkv_b), "wqc": f(w_qkv_c), "sink": f(sink_b), "qn": f(q_norm_c),
        "kn": f(k_norm_c), "gfin": f(final_norm_g), "rpbt": _rpb_tables(f(rpb_a)),
        "ident": np.eye(128, dtype=np.float32), "ropec": ropec, "ropes": ropes, "bandm": bandm,
    }
    xp, xs = f(x_prompt), f(x_sample)
    in_maps = []
    for i in range(NCORES):
        mp = dict(shared)
        mp["xp"] = xp[4 * i:4 * i + 4].reshape(1024, D)
        mp["xs"] = xs[i]
        mp["cka"] = f(cache_k_a)[i].reshape(2, 512, 1024); mp["cva"] = f(cache_v_a)[i].reshape(2, 512, 1024)
        mp["ckb"] = f(cache_k_b)[i].reshape(1, 512, 256); mp["cvb"] = f(cache_v_b)[i].reshape(1, 512, 256)
        mp["ckc"] = f(cache_k_c)[i].reshape(1, 512, 256); mp["cvc"] = f(cache_v_c)[i].reshape(1, 512, 256)
        mp["cond"] = np.ascontiguousarray(np.stack([f(c_ctx), f(c)[i]], 0))
        in_maps.append(mp)
    res = run_bass_kernel_spmd(nc, in_maps, core_ids=list(range(NCORES)))
    R = res.results
    cat = lambda k: np.concatenate([np.asarray(R[i][k], np.float32) for i in range(NCORES)], 0)
    y_prompt = cat("yp").reshape(32, 256, D)
    y_sample = np.stack([np.asarray(R[i]["ys"], np.float32) for i in range(NCORES)], 0)
    k_a = cat("ka").reshape(32, 2, 256, 16, 64); v_a = cat("va").reshape(32, 2, 256, 16, 64)
    k_b = cat("kb").reshape(32, 1, 256, 4, 64); v_b = cat("vb").reshape(32, 1, 256, 4, 64)
    k_c = cat("kc").reshape(32, 1, 256, 4, 64); v_c = cat("vc").reshape(32, 1, 256, 4, 64)
    return (y_prompt, y_sample, k_a, v_a, k_b, v_b, k_c, v_c)
```
